# Optimizing a Trainium2 kernel written in Bass

```python
import jax, jax.numpy as jnp
from jax import lax
import numpy as np

D_MODEL = 1024
BATCH = 16
SEQ = 256
DEPTH = 2
DEC_BATCH = 8
DEC_SEQ = 1024
PAST_LEN = 512

GRID_W = 64
N_EVEN = (DEPTH + 1) // 2
N_ODD = DEPTH // 2
POOL_WINDOWS = (2, 4, 8, 16)
POOL_GROUPS = 4
POOL_CH = 128
POOL_WIDTH = POOL_GROUPS * POOL_CH
MLA_HEADS = 8
QK_NOPE = 64
QK_ROPE = 32
QK_DIM = QK_NOPE + QK_ROPE
V_DIM = 64
Q_RANK = 384
KV_RANK = 256
MLA_WIDTH = MLA_HEADS * V_DIM
AB_IN = POOL_WIDTH + Q_RANK + KV_RANK + QK_ROPE
AB_OUT = POOL_WIDTH + MLA_WIDTH
CHUNK = 128
C_GROUPS = 8
C_WIDTH = D_MODEL
C_CH = C_WIDTH // C_GROUPS
D_FF = -(-8 * D_MODEL // (3 * 256)) * 256
ROPE_BASE = 10000.0
EPS = 1e-6
Q_BLOCK = 128

kernel_name = "hybrid_pool_mla_gmlp_diffusion_step"


def rms_norm(x, g):
    x32 = x.astype(jnp.float32)
    y = x32 * lax.rsqrt(jnp.mean(x32 * x32, axis=-1, keepdims=True) + EPS)
    return (y * g.astype(jnp.float32)).astype(x.dtype)


def ada_terms(cond, w, b):
    mod = (jax.nn.silu(cond) @ w + b)[:, None, :]
    return jnp.split(mod, 6, axis=-1)


def swiglu(h, wg, wu, wd):
    return (jax.nn.silu(h @ wg) * (h @ wu)) @ wd


def multi_scale_pool(u, w_pool, scale):
    B, L, _ = u.shape
    ug = u.reshape(B, L, POOL_GROUPS, POOL_CH).astype(jnp.float32)
    csum = jnp.concatenate([jnp.zeros((B, 1, POOL_GROUPS, POOL_CH), jnp.float32), jnp.cumsum(ug, axis=1)], axis=1)
    t = jnp.arange(L)
    outs = []
    for g, w in enumerate(POOL_WINDOWS):
        lo = jnp.clip(t - w // 2, 0, L)
        hi = jnp.clip(t + w // 2, 0, L)
        s = csum[:, hi, g] - csum[:, lo, g]
        outs.append(s / (hi - lo).astype(jnp.float32)[None, :, None] - ug[:, :, g])
    p = jnp.stack(outs, axis=2).astype(u.dtype)
    y = jnp.einsum('blgc,gcd->blgd', p, w_pool).reshape(B, L, POOL_WIDTH)
    return y * scale


def axial_rope_tables(L):
    rows = L // GRID_W
    row = jnp.repeat(jnp.arange(rows), GRID_W).astype(jnp.float32)
    col = jnp.tile(jnp.arange(GRID_W), rows).astype(jnp.float32)
    per_axis = QK_ROPE // 2
    inv = 1.0 / (ROPE_BASE ** (jnp.arange(0, per_axis, 2, dtype=jnp.float32) / per_axis))
    ang = jnp.concatenate([row[:, None] * inv, col[:, None] * inv], axis=-1)
    return jnp.cos(ang), jnp.sin(ang)


def rope_part(x, cos, sin):
    xn, xr = x[..., :QK_NOPE], x[..., QK_NOPE:]
    half = QK_ROPE // 2
    x1, x2 = xr[..., :half], xr[..., half:]
    cs, sn = cos[None, :, None, :], sin[None, :, None, :]
    rot = jnp.concatenate([x1 * cs - x2 * sn, x1 * sn + x2 * cs], axis=-1).astype(x.dtype)
    return jnp.concatenate([xn, rot], axis=-1)


def attend(q, k, v):
    B, Lq, H, Dk = q.shape
    nb = Lq // Q_BLOCK
    qb = q.reshape(B, nb, Q_BLOCK, H, Dk).transpose(1, 0, 2, 3, 4)
    scale = Dk ** -0.5

    def block(qi):
        s = jnp.einsum('bqhd,bkhd->bhqk', qi, k).astype(jnp.float32) * scale
        p = jax.nn.softmax(s, axis=-1).astype(v.dtype)
        return jnp.einsum('bhqk,bkhd->bqhd', p, v)

    o = lax.map(block, qb)
    return o.transpose(1, 0, 2, 3, 4).reshape(B, Lq, H * v.shape[-1])


def ab_split(h, w_in, kv_norm_g):
    z = h @ w_in
    o1 = POOL_WIDTH
    o2 = o1 + Q_RANK
    o3 = o2 + KV_RANK
    return z[..., :o1], z[..., o1:o2], rms_norm(z[..., o2:o3], kv_norm_g), z[..., o3:]


def mla_q(cq_raw, q_norm_g, w_uq, qn_g):
    B, L, _ = cq_raw.shape
    q = (rms_norm(cq_raw, q_norm_g) @ w_uq).reshape(B, L, MLA_HEADS, QK_DIM)
    return rms_norm(q, qn_g)


def mla_kv(ckv, krope, w_ukv, kn_g):
    B, L, _ = ckv.shape
    kv = (ckv @ w_ukv).reshape(B, L, MLA_HEADS, QK_NOPE + V_DIM)
    k_nope, v = kv[..., :QK_NOPE], kv[..., QK_NOPE:]
    k_pe = jnp.broadcast_to(krope[:, :, None, :], (B, L, MLA_HEADS, QK_ROPE)).astype(k_nope.dtype)
    k = rms_norm(jnp.concatenate([k_nope, k_pe], axis=-1), kn_g)
    return k, v


def mixer_ab_context(h, w_in, pool_w, pool_scale, q_norm_g, kv_norm_g, w_uq, w_ukv, qn_g, kn_g, w_out):
    u, cq_raw, ckv, krope = ab_split(h, w_in, kv_norm_g)
    y_pool = multi_scale_pool(u, pool_w, pool_scale)
    q = mla_q(cq_raw, q_norm_g, w_uq, qn_g)
    k, v = mla_kv(ckv, krope, w_ukv, kn_g)
    y_att = attend(q, k, v)
    return jnp.concatenate([y_pool, y_att], axis=-1) @ w_out, ckv, krope


def mixer_ab_latent(h, ckv_ctx, krope_ctx, w_in, pool_w, pool_scale, q_norm_g, kv_norm_g, w_uq, w_ukv, qn_g, kn_g, w_out):
    L = h.shape[1]
    cos, sin = axial_rope_tables(L)
    u, cq_raw, ckv, krope = ab_split(h, w_in, kv_norm_g)
    y_pool = multi_scale_pool(u, pool_w, pool_scale)
    q = rope_part(mla_q(cq_raw, q_norm_g, w_uq, qn_g), cos, sin)
    k_lat, v_lat = mla_kv(ckv, krope, w_ukv, kn_g)
    k_lat = rope_part(k_lat, cos, sin)
    k_ctx, v_ctx = mla_kv(ckv_ctx.astype(h.dtype), krope_ctx.astype(h.dtype), w_ukv, kn_g)
    k = jnp.concatenate([k_ctx, k_lat], axis=1)
    v = jnp.concatenate([v_ctx, v_lat], axis=1)
    y_att = attend(q, k, v)
    return jnp.concatenate([y_pool, y_att], axis=-1) @ w_out


def chunk_gmlp(h, w_in, vnorm_g, w_s, b_s, w_out):
    B, L, _ = h.shape
    z = h @ w_in
    u, v = z[..., :C_WIDTH], z[..., C_WIDTH:]
    v = rms_norm(v, vnorm_g)
    n = L // CHUNK
    vc = v.reshape(B, n, CHUNK, C_GROUPS, C_CH)
    s = jnp.einsum('gpq,bnqgc->bnpgc', w_s, vc) + b_s.T[None, None, :, :, None]
    return (u * s.reshape(B, L, C_WIDTH)) @ w_out


def setup_inputs(seed: int = 0) -> dict:
    key = jax.random.key(seed)
    ks = iter(jax.random.split(key, 40))
    nrm = lambda shape, s: jax.random.normal(next(ks), shape, jnp.float32) * s
    gain = lambda shape: 1.0 + 0.02 * jax.random.normal(next(ks), shape, jnp.float32)
    return {
        "x_prompt": nrm((BATCH, SEQ, D_MODEL), 1.0),
        "x_sample": nrm((DEC_BATCH, DEC_SEQ, D_MODEL), 1.0),
        "cache_ckv": nrm((DEC_BATCH, N_EVEN, PAST_LEN, KV_RANK), 1.0),
        "cache_krope": nrm((DEC_BATCH, N_EVEN, PAST_LEN, QK_ROPE), 1.0),
        "c": nrm((DEC_BATCH, D_MODEL), 1.0),
        "c_ctx": nrm((D_MODEL,), 1.0),
        "ada_w": nrm((DEPTH, D_MODEL, 6 * D_MODEL), 0.5 * D_MODEL ** -0.5),
        "ada_b": nrm((DEPTH, 6 * D_MODEL), 0.02),
        "norm_mix_g": gain((DEPTH, D_MODEL)),
        "norm_ffn_g": gain((DEPTH, D_MODEL)),
        "ffn_wg": nrm((DEPTH, D_MODEL, D_FF), D_MODEL ** -0.5),
        "ffn_wu": nrm((DEPTH, D_MODEL, D_FF), D_MODEL ** -0.5),
        "ffn_wd": nrm((DEPTH, D_FF, D_MODEL), D_FF ** -0.5),
        "ab_w_in": nrm((N_EVEN, D_MODEL, AB_IN), D_MODEL ** -0.5),
        "pool_w": nrm((N_EVEN, POOL_GROUPS, POOL_CH, POOL_CH), POOL_CH ** -0.5),
        "pool_scale": gain((N_EVEN, POOL_WIDTH)),
        "q_norm_g": gain((N_EVEN, Q_RANK)),
        "kv_norm_g": gain((N_EVEN, KV_RANK)),
        "w_uq": nrm((N_EVEN, Q_RANK, MLA_HEADS * QK_DIM), Q_RANK ** -0.5),
        "w_ukv": nrm((N_EVEN, KV_RANK, MLA_HEADS * (QK_NOPE + V_DIM)), KV_RANK ** -0.5),
        "qn_g": gain((N_EVEN, QK_DIM)),
        "kn_g": gain((N_EVEN, QK_DIM)),
        "ab_w_out": nrm((N_EVEN, AB_OUT, D_MODEL), AB_OUT ** -0.5),
        "gm_w_in": nrm((N_ODD, D_MODEL, 2 * C_WIDTH), D_MODEL ** -0.5),
        "gm_vnorm_g": gain((N_ODD, C_WIDTH)),
        "gm_ws": nrm((N_ODD, C_GROUPS, CHUNK, CHUNK), CHUNK ** -0.5),
        "gm_bs": nrm((N_ODD, C_GROUPS, CHUNK), 0.02),
        "gm_w_out": nrm((N_ODD, C_WIDTH, D_MODEL), C_WIDTH ** -0.5),
    }


def reference(x_prompt, x_sample, cache_ckv, cache_krope, c, c_ctx, ada_w, ada_b, norm_mix_g, norm_ffn_g, ffn_wg, ffn_wu, ffn_wd, ab_w_in, pool_w, pool_scale, q_norm_g, kv_norm_g, w_uq, w_ukv, qn_g, kn_g, ab_w_out, gm_w_in, gm_vnorm_g, gm_ws, gm_bs, gm_w_out):
    ctx = x_prompt
    lat = x_sample
    new_ckv, new_krope = [], []
    for i in range(DEPTH):
        csh1, csc1, cg1, csh2, csc2, cg2 = ada_terms(c_ctx[None, :], ada_w[i], ada_b[i])
        lsh1, lsc1, lg1, lsh2, lsc2, lg2 = ada_terms(c, ada_w[i], ada_b[i])
        h_ctx = rms_norm(ctx, norm_mix_g[i]) * (1 + csc1) + csh1
        h_lat = rms_norm(lat, norm_mix_g[i]) * (1 + lsc1) + lsh1
        if i % 2 == 0:
            e = i // 2
            prm = (ab_w_in[e], pool_w[e], pool_scale[e], q_norm_g[e], kv_norm_g[e], w_uq[e], w_ukv[e], qn_g[e], kn_g[e], ab_w_out[e])
            y_ctx, ckv, krope = mixer_ab_context(h_ctx, *prm)
            y_lat = mixer_ab_latent(h_lat, cache_ckv[:, e], cache_krope[:, e], *prm)
            new_ckv.append(ckv)
            new_krope.append(krope)
        else:
            o = i // 2
            prm = (gm_w_in[o], gm_vnorm_g[o], gm_ws[o], gm_bs[o], gm_w_out[o])
            y_ctx = chunk_gmlp(h_ctx, *prm)
            y_lat = chunk_gmlp(h_lat, *prm)
        ctx = ctx + cg1 * y_ctx
        lat = lat + lg1 * y_lat
        h_ctx = rms_norm(ctx, norm_ffn_g[i]) * (1 + csc2) + csh2
        h_lat = rms_norm(lat, norm_ffn_g[i]) * (1 + lsc2) + lsh2
        ctx = ctx + cg2 * swiglu(h_ctx, ffn_wg[i], ffn_wu[i], ffn_wd[i])
        lat = lat + lg2 * swiglu(h_lat, ffn_wg[i], ffn_wu[i], ffn_wd[i])
    state_ckv = jnp.stack(new_ckv, axis=1)
    state_krope = jnp.stack(new_krope, axis=1)
    return (ctx, lat, state_ckv, state_krope)
```

```python
import numpy as np
from contextlib import ExitStack
import concourse.bass as bass
import concourse.mybir as mybir
from concourse.bass_utils import run_bass_kernel_spmd

F32 = mybir.dt.float32
BF16 = mybir.dt.bfloat16
AF = mybir.ActivationFunctionType
ALU = mybir.AluOpType
AX = mybir.AxisListType

EPS = 1e-6
T = 1536
NV = 163
SEQS = [(0, 256), (256, 256), (512, 1024)]
PADOFF = [0, 272, 544]
TP = 1584


def _esize(dt):
    return 2 if dt == BF16 else 4


class Prog:
    def __init__(self, nc, es):
        self.nc = nc
        self.E = dict(pe=nc.tensor, act=nc.scalar, dve=nc.vector, pool=nc.gpsimd, sp=nc.sync)
        self.sem = {}
        self.cnt = {}
        self.es = es
        for e in ("pe", "act", "dve", "pool"):
            self.sem["c_" + e] = es.enter_context(nc.semaphore("c_" + e))
            self.cnt[e] = 0
        self.known = {e: {} for e in self.E}
        self.recs = {}
        self.freed = []
        self.dtot = {}
        self.dlast = {}
        self.banks = []
        self.bank_i = {}
        self.nins = 0

    def sb(self, es, name, shape, dt):
        self.uid = getattr(self, "uid", 0) + 1
        name = "%s_%d" % (name, self.uid)
        t = es.enter_context(self.nc.sbuf_tensor(name, list(shape), dt))
        addr = self.nc.lookup_mloc(t).addr
        fsz = int(np.prod(shape[1:]))
        nbytes = fsz * _esize(dt)
        rec = {"w": [], "r": []}
        for (a0, a1, evd) in self.freed:
            if a0 < addr + nbytes and addr < a1:
                for s, v in evd.items():
                    rec["w"].append((0, 128, 0, fsz, s, v))
        self.recs[name] = rec
        es.callback(self._free, name, addr, nbytes)
        return t

    def _free(self, name, addr, nbytes):
        rec = self.recs.pop(name)
        evd = {}
        for x in rec["w"] + rec["r"]:
            if x[5] > evd.get(x[4], 0):
                evd[x[4]] = x[5]
        self.freed.append((addr, addr + nbytes, evd))

    def psum_banks(self, es):
        for i in range(8):
            t = es.enter_context(self.nc.psum_tensor("pb%d" % i, [128, 512], F32))
            self.recs["pb%d" % i] = {"w": [], "r": []}
            self.banks.append(t)

    def bank(self, pool=(0, 1, 2, 3, 4, 5, 6, 7)):
        i = self.bank_i.get(pool, 0)
        self.bank_i[pool] = (i + 1) % len(pool)
        return self.banks[pool[i]]

    def dsem(self, key):
        k = "d_" + key
        if k not in self.sem:
            self.sem[k] = self.es.enter_context(self.nc.semaphore(k))
            self.dtot[k] = 0
        return k

    @staticmethod
    def reg(a):
        sp = str(a.space)
        if "DRAM" in sp:
            return None
        if "PSUM" in sp:
            return (a.tensor.name, 0, 128, 0, 512)
        ap = a.ap
        ps = ap[0][0]
        off = a.offset
        if ps == 0:
            p0, f0 = 0, off
        else:
            p0 = off // ps
            f0 = off - p0 * ps
        ext = 1
        for st, c in ap[1:]:
            ext += (c - 1) * st
        return (a.tensor.name, p0, p0 + ap[0][1], f0, f0 + ext)

    def _deps(self, rr, wr, e=None):
        evs = []
        for r in rr:
            for w in self.recs[r[0]]["w"]:
                if w[0] < r[2] and r[1] < w[1] and w[2] < r[4] and r[3] < w[3]:
                    evs.append((w[4], w[5]))
            if r[0].startswith("pb"):
                for w in self.recs[r[0]]["r"]:
                    if w[4] != "c_" + str(e):
                        evs.append((w[4], w[5]))
        for r in wr:
            rec = self.recs[r[0]]
            for w in rec["w"]:
                if w[0] < r[2] and r[1] < w[1] and w[2] < r[4] and r[3] < w[3]:
                    evs.append((w[4], w[5]))
            for w in rec["r"]:
                if w[0] < r[2] and r[1] < w[1] and w[2] < r[4] and r[3] < w[3]:
                    evs.append((w[4], w[5]))
        return evs

    def _record(self, rr, wr, ev):
        s, v = ev
        for r in rr:
            lst = self.recs[r[0]]["r"]
            key = r[1:]
            for i, x in enumerate(lst):
                if x[4] == s and x[:4] == key:
                    lst[i] = key + (s, v)
                    break
            else:
                lst.append(key + (s, v))
        for r in wr:
            rec = self.recs[r[0]]
            for nm in ("w", "r"):
                rec[nm] = [x for x in rec[nm]
                           if not (r[1] <= x[0] and x[1] <= r[2] and r[3] <= x[2] and x[3] <= r[4])]
            rec["w"].append(r[1:] + (s, v))

    def _waits(self, e, evs):
        need = {}
        for s, v in evs:
            if s in self.dtot:
                v = max(v, self.dtot[s])
            if v > need.get(s, 0):
                need[s] = v
        k = self.known[e]
        out = []
        for s, v in need.items():
            if e == "pe" and s == "c_pe":
                continue
            if k.get(s, 0) < v:
                k[s] = v
                out.append((s, v))
        return out

    def op(self, e, fn, reads, writes, inc=True):
        rr = [x for x in (self.reg(a) for a in reads) if x is not None]
        wr = [x for x in (self.reg(a) for a in writes) if x is not None]
        waits = self._waits(e, self._deps(rr, wr, e))
        eng = self.E[e]
        attach = (e != "pe")
        for s, v in (waits[:-1] if attach else waits):
            eng.wait_ge(self.sem[s], v)
        ins = fn(eng)
        if waits and attach:
            ins._wait_ge(self.sem[waits[-1][0]], waits[-1][1])
        if inc:
            ins.then_inc(self.sem["c_" + e], 1)
            self.cnt[e] += 1
            ev = ("c_" + e, self.cnt[e])
        else:
            ev = ("c_" + e, self.cnt[e] + 1)
        self._record(rr, wr, ev)
        self.nins += 1
        return ins

    def dma(self, q, out, in_, key, slow=False):
        rr = [x for x in (self.reg(in_),) if x is not None]
        wr = [x for x in (self.reg(out),) if x is not None]
        waits = self._waits(q, self._deps(rr, wr))
        eng = self.E[q]
        for s, v in waits:
            eng.wait_ge(self.sem[s], v)
        k = self.dsem(key)
        side = out if wr else in_
        nm = side.tensor.name
        last = self.dlast.get(k)
        if last is not None and last != nm and self.known[q].get(k, 0) < self.dtot[k]:
            eng.wait_ge(self.sem[k], self.dtot[k])
            self.known[q][k] = self.dtot[k]
        self.dlast[k] = nm
        if slow:
            ins = eng.dma_start(out=out, in_=in_, allow_slow_non_contiguous=True)
        else:
            ins = eng.dma_start(out=out, in_=in_)
        ins.then_inc(self.sem[k], 16)
        self.dtot[k] += 16
        self._record(rr, wr, (k, self.dtot[k]))
        return ins

    def mm(self, out, lhsT, rhs, start=True, stop=True, inc=True, sgc=False):
        if sgc:
            return self.op("pe", lambda g: g.matmul(out, lhsT, rhs, start=start, stop=stop, skip_group_check=True),
                           [lhsT, rhs], [out], inc=inc)
        return self.op("pe", lambda g: g.matmul(out, lhsT, rhs, start=start, stop=stop),
                       [lhsT, rhs], [out], inc=inc)

    def mmg(self, out, pairs):
        n = len(pairs)
        for i, (l, r) in enumerate(pairs):
            self.mm(out, l, r, start=(i == 0), stop=(i == n - 1), inc=(i == n - 1))

    def tr(self, out, in_, ident):
        return self.op("pe", lambda g: g.transpose(out, in_, ident), [in_, ident], [out])

    def act(self, out, in_, func, scale=1.0, bias=0.0):
        reads = [in_]
        if not isinstance(scale, (int, float)):
            reads.append(scale)
        if not isinstance(bias, (int, float)):
            reads.append(bias)
        return self.op("act", lambda g: g.activation(out=out, in_=in_, func=func, bias=bias, scale=scale),
                       reads, [out])

    def tt(self, e, out, in0, in1, op):
        return self.op(e, lambda g: g.tensor_tensor(out=out, in0=in0, in1=in1, op=op), [in0, in1], [out])

    def ts(self, e, out, in0, s1, s2, op0, op1=None):
        reads = [in0]
        for s in (s1, s2):
            if s is not None and not isinstance(s, (int, float)):
                reads.append(s)
        if op1 is None:
            return self.op(e, lambda g: g.tensor_scalar(out=out, in0=in0, scalar1=s1, scalar2=None, op0=op0),
                           reads, [out])
        return self.op(e, lambda g: g.tensor_scalar(out=out, in0=in0, scalar1=s1, scalar2=s2, op0=op0, op1=op1),
                       reads, [out])

    def stt(self, e, out, in0, sc, in1, op0, op1):
        reads = [in0, in1]
        if not isinstance(sc, (int, float)):
            reads.append(sc)
        return self.op(e, lambda g: g.scalar_tensor_tensor(out=out, in0=in0, scalar=sc, in1=in1, op0=op0, op1=op1),
                       reads, [out])

    def cp(self, e, out, in_):
        if e == "act":
            return self.op("act", lambda g: g.copy(out=out, in_=in_), [in_], [out])
        return self.op(e, lambda g: g.tensor_copy(out=out, in_=in_), [in_], [out])

    def memset(self, e, out, val):
        return self.op(e, lambda g: g.memset(out, val), [], [out])

    def recip(self, out, in_):
        return self.op("dve", lambda g: g.reciprocal(out=out, in_=in_), [in_], [out])


class Ring:
    def __init__(self, tiles):
        self.t = tiles
        self.i = 0

    def next(self):
        t = self.t[self.i]
        self.i = (self.i + 1) % len(self.t)
        return t


def wview(ap2d):
    return ap2d.rearrange("(kc p) n -> p kc n", p=128)


class _Stop(Exception):
    pass


def build(dbg=(), stop_after=None):
    nc = bass.Bass("TRN2", target_bir_lowering=False)

    def din(name, shape):
        return nc.dram_tensor(name, list(shape), F32, kind="ExternalInput").ap()

    def dout(name, shape):
        return nc.dram_tensor(name, list(shape), F32, kind="ExternalOutput").ap()

    x_in = din("x_in", [T, 1024])
    cckv = din("cckv", [512, 256])
    ckr = din("ckr", [512, 32])
    vecs_d = din("vecs", [128, NV])
    ident_d = din("ident", [128, 128])
    rope_d = din("rope", [2, 96, 1024])
    srot_d = din("srot", [96, 96])
    eplace_d = din("eplace", [32, 96])
    invcnt_d = din("invcnt", [4, T])
    bsbc_d = din("bsbc", [128, 1024])
    gvbc_d = din("gvbc", [128, 1024])
    wsT_d = din("wsT", [128, 8, 128])
    ada_w = din("ada_w", [2, 1024, 6144])
    ffn_wg = din("ffn_wg", [2, 1024, 2816])
    ffn_wu = din("ffn_wu", [2, 1024, 2816])
    ffn_wd = din("ffn_wd", [2, 2816, 1024])
    ab_w_in = din("ab_w_in", [1024, 1184])
    pool_w = din("pool_w", [4, 128, 128])
    w_uq = din("w_uq", [384, 768])
    w_ukv = din("w_ukv", [256, 1024])
    ab_w_out = din("ab_w_out", [1024, 1024])
    gm_w_in = din("gm_w_in", [1024, 2048])
    gm_w_out = din("gm_w_out", [1024, 1024])

    y_out = dout("y", [T, 1024])
    sckv_out = dout("sckv", [512, 256])
    skr_out = dout("skr", [512, 32])
    dbg_out = {}

    with ExitStack() as es:
        P = Prog(nc, es)
        P.psum_banks(es)

        def dbgdump(name, ap, shape):
            if name in dbg:
                d = nc.dram_tensor("dbg_" + name, list(shape), ap.dtype, kind="ExternalOutput").ap()
                dbg_out[name] = d
                P.dma("sp", d, ap, "dbg")
            if name == stop_after:
                raise _Stop()

        try:
            xT = P.sb(es, "xT", [128, 8, T], F32)
            vecs = P.sb(es, "vecs", [128, NV], F32)
            ident = P.sb(es, "ident", [128, 128], F32)
            ones = P.sb(es, "ones", [128, 128], BF16)
            mod = P.sb(es, "mod", [128, 2, 48, 2], F32)
            AA = P.sb(es, "AA", [128, 2, 2, 8, 2], F32)
            gsc = P.sb(es, "gsc", [128, 8], F32)
            epsc = P.sb(es, "epsc", [128, 5], F32)
            EPSI = {1024: 0, 384: 1, 256: 2, 96: 3}
            for n_, i_ in EPSI.items():
                P.memset("dve", epsc[:, i_:i_ + 1], float(n_) * EPS)
            P.memset("dve", epsc[:, 4:5], float(np.log(32.0)))

            def rstd(out, in_, n_, np_=128):
                P.act(out, in_, AF.Ln, bias=epsc[0:np_, EPSI[n_]:EPSI[n_] + 1])
                P.act(out, out, AF.Exp, scale=-0.5)

            P.dma("sp", vecs[:], vecs_d, "const")
            P.dma("sp", ident[:], ident_d, "const")
            P.memset("dve", ones[:], 1.0)
            P.ts("dve", gsc[:, 0:3], vecs[:, 132:135], float(np.sqrt(384.0)), None, ALU.mult)
            P.ts("dve", gsc[:, 3:5], vecs[:, 135:137], 16.0, None, ALU.mult)
            P.ts("dve", gsc[:, 5:7], vecs[:, 137:139], float(np.sqrt(96.0)), None, ALU.mult)

            def blk(b):
                return slice(b * 512, (b + 1) * 512)

            def load_transposed(src, ntile, F, dst3, evac_engs=("act", "dve"), nring=2, per_tile=None):
                nfc = (F + 127) // 128
                with ExitStack() as es2:
                    ring = Ring([P.sb(es2, "ldt%d" % i, [128, F], F32) for i in range(nring)])
                    k = 0
                    for i in range(ntile):
                        st = ring.next()
                        P.dma("sp", st[:], src[i * 128:(i + 1) * 128, :], "ldt%d" % (i % nring))
                        for f0 in range(0, nfc, 4):
                            nb = min(4, nfc - f0)
                            fp = min(128, F - f0 * 128)
                            ps = P.bank()
                            for j in range(nb):
                                fc = f0 + j
                                P.tr(ps[0:fp, j * 128:(j + 1) * 128], st[:, fc * 128:fc * 128 + fp], ident[:])
                            P.cp(evac_engs[k % 2], dst3(f0, nb, i, fp),
                                 ps[0:fp, 0:nb * 128].rearrange("p (a b) -> p a b", b=128))
                            k += 1
                        if per_tile is not None:
                            per_tile(i)

            scT = P.sb(es, "scT", [128, 16], BF16)
            P.act(scT[:], vecs[:, 147:163], AF.Silu)
            esAda = ExitStack()
            ada_slots = [P.sb(esAda, "adaw%d" % i, [128, 8, 256], BF16) for i in range(3)]
            ada_state = {"done": 0}
            aa_done = set()

            def ada_piece(pool=None):
                k = ada_state["done"]
                if k >= 48:
                    return
                ada_state["done"] = k + 1
                l, c0 = divmod(2 * k, 48)
                sl = ada_slots[k % 3]
                P.dma("pool", sl[:], wview(ada_w[l])[:, :, c0 * 128:(c0 + 2) * 128], "adaw%d" % (k % 3))
                psb = P.bank(pool) if pool else P.bank()
                for m in range(2):
                    P.mmg(psb[:, 2 * m:2 * m + 2],
                          [(sl[:, kc, m * 128:(m + 1) * 128], scT[:, 2 * kc:2 * kc + 2]) for kc in range(8)])
                bias_bc = bass.AP(vecs, 32 + l * 48 + c0, [[NV, 128], [1, 2], [0, 2]])
                P.tt("dve", mod[:, l, c0:c0 + 2, :], psb[:, 0:4].rearrange("p (c k) -> p c k", k=2), bias_bc, ALU.add)

            def ada_ensure(n):
                while ada_state["done"] < n:
                    ada_piece()

            def ensure_AA(l, w):
                if (l, w) in aa_done:
                    return
                aa_done.add((l, w))
                term = 1 if w == 0 else 4
                ada_ensure(l * 24 + (term * 8 + 8) // 2)
                P.ts("dve", AA[:, l, w], mod[:, l, term * 8:(term + 1) * 8, :], 1.0, 32.0, ALU.add, ALU.mult)
                gcol = (0 if w == 0 else 16) + l * 8
                g_bc = bass.AP(vecs, gcol, [[NV, 128], [1, 8], [0, 2]])
                P.tt("dve", AA[:, l, w], AA[:, l, w], g_bc, ALU.mult)

            load_transposed(x_in, 12, 1024, lambda f0, nb, i, fp: xT[:, f0:f0 + nb, i * 128:(i + 1) * 128], nring=4,
                            per_tile=lambda i: (ada_piece() if ada_state["done"] < 8 else None))
            ensure_AA(0, 0)
            dbgdump("xT0", xT[:, 0, :], [128, T])

            def mscal(l, term, kc, cond):
                return mod[:, l, term * 8 + kc, cond:cond + 1]

            def xnorm(es2, l, w, hT, small=False):
                ensure_AA(l, w)
                sqr = Ring([P.sb(es2, "xsq%d" % i, [128, 512], BF16) for i in range(2 if small else 4)])
                rsr = Ring([P.sb(es2, "xrs%d" % i, [128, 512], F32) for i in range(1 if small else 3)])
                tmpr = Ring([P.sb(es2, "xtmp%d" % i, [128, 512], F32) for i in range(2 if small else 4)])
                shterm = 0 if w == 0 else 3
                for b in range(3):
                    cond = 0 if b == 0 else 1
                    ps = P.bank()
                    rs = rsr.next()
                    for kc in range(8):
                        sq = sqr.next()
                        if kc % 2 == 0:
                            P.act(sq[:], xT[:, kc, blk(b)], AF.Square)
                        else:
                            P.tt("dve", sq[:], xT[:, kc, blk(b)], xT[:, kc, blk(b)], ALU.mult)
                        P.mm(ps[:], ones[:], sq[:], start=(kc == 0), stop=(kc == 7))
                    rstd(rs[:], ps[:], 1024)
                    for kc in range(8):
                        tmp = tmpr.next()
                        P.tt("dve", tmp[:], xT[:, kc, blk(b)], rs[:], ALU.mult)
                        if kc % 2 == 0:
                            P.act(hT[:, kc, blk(b)], tmp[:], AF.Identity,
                                  scale=AA[:, l, w, kc, cond:cond + 1], bias=mscal(l, shterm, kc, cond))
                        else:
                            P.ts("pool", hT[:, kc, blk(b)], tmp[:], AA[:, l, w, kc, cond:cond + 1],
                                 mscal(l, shterm, kc, cond), ALU.mult, ALU.add)

            def residual_update(ps, l, term, m, b):
                cond = 0 if b == 0 else 1
                P.stt("dve", xT[:, m, blk(b)], ps[:], mscal(l, term, m, cond), xT[:, m, blk(b)], ALU.mult, ALU.add)

            def ffn(l):
                ada_ensure(l * 24 + 24)
                with ExitStack() as es2:
                    wgs = [P.sb(es2, "wgs%d" % i, [128, 8, 256], BF16) for i in range(3)]
                    wus = [P.sb(es2, "wus%d" % i, [128, 8, 256], BF16) for i in range(3)]
                    wgv = wview(ffn_wg[l])
                    wuv = wview(ffn_wu[l])
                    wdv = wview(ffn_wd[l])
                    issued = {}

                    def issue(k):
                        if k in issued or k > 10:
                            return
                        jp = 2 * k
                        wg_s, wu_s = wgs[k % 3], wus[k % 3]
                        P.dma("pool", wg_s[:], wgv[:, :, jp * 128:(jp + 2) * 128], "wg%d" % (k % 3))
                        P.dma("pool", wu_s[:], wuv[:, :, jp * 128:(jp + 2) * 128], "wu%d" % (k % 3))
                        issued[k] = (wg_s, wu_s)

                    issue(0)
                    issue(1)
                    hT = P.sb(es2, "hT", [128, 8, T], BF16)
                    with ExitStack() as es3:
                        xnorm(es3, l, 1, hT)
                    groups = [(0, 6), (6, 12), (12, 18), (18, 22)]
                    aparts = [P.sb(es2, "apart%d" % i, [128, 6, T], BF16) for i in range(2)]
                    wdb = [P.sb(es2, "wdb%d" % i, [128, 6, 1024], BF16) for i in range(2)]
                    sgr = Ring([P.sb(es2, "sg%d" % i, [128, 512], F32) for i in range(3)])

                    def up(gi, after_first=None):
                        j0, j1 = groups[gi]
                        ap_ = aparts[gi % 2]
                        for jp in range(j0, j1, 2):
                            k = jp // 2
                            issue(k)
                            issue(k + 1)
                            issue(k + 2)
                            wg_s, wu_s = issued[k]
                            if jp == j0 and after_first is not None:
                                after_first()
                            for jj in range(2):
                                j = jp + jj
                                for b in range(3):
                                    psg = P.bank()
                                    psu = P.bank()
                                    P.mmg(psg[:], [(wg_s[:, kc, jj * 128:(jj + 1) * 128], hT[:, kc, blk(b)]) for kc in range(8)])
                                    P.mmg(psu[:], [(wu_s[:, kc, jj * 128:(jj + 1) * 128], hT[:, kc, blk(b)]) for kc in range(8)])
                                    sg = sgr.next()
                                    P.act(sg[:], psg[:], AF.Silu)
                                    P.tt("dve", ap_[:, j - j0, blk(b)], psu[:], sg[:], ALU.mult)

                    def load_wd(gi):
                        j0, j1 = groups[gi]
                        wb = wdb[gi % 2]
                        for jp in range(j0, j1, 2):
                            P.dma("pool", wb[:, jp - j0:jp - j0 + 2, :], wdv[:, jp:jp + 2, :], "wd%d" % (gi % 2))

                    def down(gi):
                        j0, j1 = groups[gi]
                        ap_ = aparts[gi % 2]
                        wb = wdb[gi % 2]
                        for b in range(3):
                            for m in range(8):
                                ps = P.bank()
                                P.mmg(ps[:], [(wb[:, jj, m * 128:(m + 1) * 128], ap_[:, jj, blk(b)]) for jj in range(j1 - j0)])
                                residual_update(ps, l, 5, m, b)

                    up(0, lambda: load_wd(0))
                    up(1, lambda: load_wd(1))
                    down(0)
                    up(2, lambda: load_wd(2))
                    down(1)
                    up(3, lambda: load_wd(3))
                    down(2)
                    down(3)

            with ExitStack() as esL0:
                cqn = P.sb(esL0, "cqn", [128, 3, T], BF16)
                ckvn = P.sb(esL0, "ckvn", [128, 2, 2048], BF16)
                kr_bf = P.sb(esL0, "kr_bf", [32, 2048], BF16)
                yab = P.sb(esL0, "yab", [128, 8, T], BF16)

                load_transposed(cckv, 4, 256, lambda f0, nb, i, fp: ckvn[:, f0:f0 + nb, 1536 + i * 128:1536 + (i + 1) * 128])
                load_transposed(ckr, 4, 32, lambda f0, nb, i, fp: kr_bf[0:32, 1536 + i * 128:1536 + (i + 1) * 128].rearrange("p (a b) -> p a b", a=1))

                with ExitStack() as esB:
                    uT = P.sb(esB, "uT", [128, 4, TP], F32)
                    P.memset("dve", uT[:], 0.0)
                    with ExitStack() as esA:
                        w_in = P.sb(esA, "w_in", [128, 8, 1184], BF16)
                        wiv = wview(ab_w_in)
                        P.dma("pool", w_in[:, 0:4, :], wiv[:, 0:4, :], "w_in")
                        P.dma("pool", w_in[:, 4:8, :], wiv[:, 4:8, :], "w_in")
                        hT = P.sb(esA, "hT", [128, 8, T], BF16)
                        with ExitStack() as es3:
                            xnorm(es3, 0, 0, hT, small=True)
                        dbgdump("h0", hT[:, 0, :], [128, T])
                        dbgdump("F_h0", hT[:].rearrange("p c t -> p (c t)"), [128, 8 * T])
                        cqf = P.sb(esA, "cqf", [128, 3, 512], F32)
                        sqb = Ring([P.sb(esA, "sqb%d" % i, [128, 512], BF16) for i in range(3)])
                        rsb = P.sb(esA, "rsb", [128, 512], F32)
                        ckvf = P.sb(esA, "ckvf", [128, 2, 512], F32)
                        krf = P.sb(esA, "krf", [32, 512], F32)
                        stg = Ring([P.sb(esA, "stg%d" % i, [128, 288], F32) for i in range(2)])

                        def proj(m, b, M=128):
                            ps = P.bank()
                            P.mmg(ps[0:M, :], [(w_in[:, kc, m * 128:m * 128 + M], hT[:, kc, blk(b)]) for kc in range(8)])
                            return ps

                        for b in range(3):
                            for m in range(4):
                                ps = proj(m, b)
                                if b == 0:
                                    for s in range(2):
                                        P.cp("act", uT[:, m, PADOFF[s] + 8:PADOFF[s] + 8 + 256], ps[:, s * 256:(s + 1) * 256])
                                else:
                                    o = PADOFF[2] + 8 + (b - 1) * 512
                                    P.cp("act", uT[:, m, o:o + 512], ps[:])
                                ada_piece()
                            pss = P.bank()
                            for j in range(3):
                                ps = proj(4 + j, b)
                                P.cp("dve", cqf[:, j, :], ps[:])
                                sq = sqb.next()
                                P.act(sq[:], ps[:], AF.Square)
                                P.mm(pss[:], ones[:], sq[:], start=(j == 0), stop=(j == 2))
                            rstd(rsb[:], pss[:], 384)
                            for j in range(3):
                                P.stt("dve", cqn[:, j, blk(b)], cqf[:, j, :], gsc[:, j:j + 1], rsb[:], ALU.mult, ALU.mult)
                            ada_piece()
                            pss = P.bank()
                            for j in range(2):
                                ps = proj(7 + j, b)
                                P.cp("dve", ckvf[:, j, :], ps[:])
                                sq = sqb.next()
                                P.act(sq[:], ps[:], AF.Square)
                                P.mm(pss[:], ones[:], sq[:], start=(j == 0), stop=(j == 1))
                            rstd(rsb[:], pss[:], 256)
                            for j in range(2):
                                P.stt("dve", ckvn[:, j, blk(b)], ckvf[:, j, :], gsc[:, 3 + j:4 + j], rsb[:], ALU.mult, ALU.mult)
                                if b == 0:
                                    P.stt("dve", ckvf[:, j, :], ckvf[:, j, :], gsc[:, 3 + j:4 + j], rsb[:], ALU.mult, ALU.mult)
                            ada_piece()
                            ps = proj(9, b, M=32)
                            P.cp("act", kr_bf[0:32, blk(b)], ps[0:32, :])
                            if b == 0:
                                P.cp("dve", krf[:], ps[0:32, :])
                                for i in range(4):
                                    pst = P.bank()
                                    for j in range(2):
                                        P.tr(pst[:, j * 128:(j + 1) * 128], ckvf[:, j, i * 128:(i + 1) * 128], ident[:])
                                    P.tr(pst[:, 256:288], krf[0:32, i * 128:(i + 1) * 128], ident[0:32, 0:32])
                                    st = stg.next()
                                    P.cp("act", st[:], pst[:, 0:288])
                                    P.dma("sp", sckv_out[i * 128:(i + 1) * 128, :], st[:, 0:256], "stout")
                                    P.dma("sp", skr_out[i * 128:(i + 1) * 128, :], st[:, 256:288], "stout")
                    dbgdump("cqn", cqn[:, 0, :], [128, T])
                    dbgdump("F_cqn", cqn[:].rearrange("p c t -> p (c t)"), [128, 3 * T])
                    dbgdump("F_ckvn", ckvn[:].rearrange("p c t -> p (c t)"), [128, 2 * 2048])
                    dbgdump("F_kr", kr_bf[:], [32, 2048])
                    dbgdump("F_uT", uT[:].rearrange("p c t -> p (c t)"), [128, 4 * TP])
                    dbgdump("ckvn", ckvn[:, 0, :], [128, 2048])
                    dbgdump("uT", uT[:, 0, :], [128, TP])

                    with ExitStack() as esP:
                        S2 = P.sb(esP, "S2", [128, TP], F32)
                        S4 = P.sb(esP, "S4", [128, TP], F32)
                        S8 = P.sb(esP, "S8", [128, TP], F32)
                        S16 = P.sb(esP, "S16", [128, TP], F32)
                        icr = Ring([P.sb(esP, "ic%d" % i, [128, T], F32) for i in range(2)])
                        pbf = P.sb(esP, "pbf", [128, 4, T], BF16)
                        pw = P.sb(esP, "pw", [128, 4, 128], BF16)
                        P.dma("pool", pw[:], pool_w.rearrange("g c d -> c g d"), "pw")
                        for g in range(4):
                            e = "dve" if g % 2 == 0 else "pool"
                            U = uT[:, g, :]
                            P.tt(e, S2[:, 1:TP], U[:, 0:TP - 1], U[:, 1:TP], ALU.add)
                            S = S2
                            if g >= 1:
                                P.tt(e, S4[:, 2:TP - 1], S2[:, 1:TP - 2], S2[:, 3:TP], ALU.add)
                                S = S4
                            if g >= 2:
                                P.tt(e, S8[:, 4:TP - 3], S4[:, 2:TP - 5], S4[:, 6:TP - 1], ALU.add)
                                S = S8
                            if g >= 3:
                                P.tt(e, S16[:, 8:TP - 7], S8[:, 4:TP - 11], S8[:, 12:TP - 3], ALU.add)
                                S = S16
                            ic = icr.next()
                            P.dma("sp", ic[:], bass.AP(invcnt_d.tensor, g * T, [[0, 128], [1, T]]), "ic%d" % (g % 2))
                            for s, (t0, L) in enumerate(SEQS):
                                o = PADOFF[s] + 8
                                P.tt(e, S[:, o:o + L], S[:, o:o + L], ic[:, t0:t0 + L], ALU.mult)
                                P.tt(e, pbf[:, g, t0:t0 + L], S[:, o:o + L], U[:, o:o + L], ALU.subtract)
                            for b in range(3):
                                ps = P.bank()
                                P.mm(ps[:], pw[:, g, :], pbf[:, g, blk(b)])
                                P.act(yab[:, g, blk(b)], ps[:], AF.Identity, scale=vecs[:, 128 + g:129 + g])
                        dbgdump("pbf", pbf[:, 3, :], [128, T])
                dbgdump("ypool", yab[:, 0, :], [128, T])

                with ExitStack() as esC:
                    wuq = P.sb(esC, "wuq", [128, 3, 768], BF16)
                    wukv = P.sb(esC, "wukv", [128, 2, 1024], BF16)
                    P.dma("pool", wuq[:], wview(w_uq), "wuq")
                    P.dma("pool", wukv[:], wview(w_ukv), "wukv")
                    ropeC = P.sb(esC, "ropeC", [96, 1024], F32)
                    ropeS = P.sb(esC, "ropeS", [96, 1024], F32)
                    P.dma("sp", ropeC[:], rope_d[0], "rope")
                    P.dma("sp", ropeS[:], rope_d[1], "rope")
                    srot_f = P.sb(esC, "srot_f", [96, 96], F32)
                    srot = P.sb(esC, "srot", [96, 96], BF16)
                    epl_f = P.sb(esC, "epl_f", [32, 96], F32)
                    epl = P.sb(esC, "epl", [32, 96], BF16)
                    P.dma("sp", srot_f[:], srot_d, "rope")
                    P.dma("sp", epl_f[:], eplace_d, "rope")
                    P.cp("dve", srot[:], srot_f[:])
                    P.cp("dve", epl[:], epl_f[:])
                    qT = P.sb(esC, "qT", [96, 4, 1024], BF16)
                    kT = P.sb(esC, "kT", [96, 4, 1536], BF16)
                    Vsb = P.sb(esC, "Vsb", [128, 12, 4, 128], BF16)
                    for hl in range(4):
                        if hl % 2 == 0:
                            P.memset("pool", Vsb[:, :, hl, 64:128], 1.0)
                        else:
                            P.memset("pool", Vsb[:, :, hl, 0:64], 1.0)
                    sqh = Ring([P.sb(esC, "sqh%d" % i, [96, 512], BF16) for i in range(4)])
                    rsh = Ring([P.sb(esC, "rsh%d" % i, [96, 512], F32) for i in range(4)])
                    t1r = Ring([P.sb(esC, "t1r%d" % i, [96, 512], F32) for i in range(3)])
                    t2r = Ring([P.sb(esC, "t2r%d" % i, [96, 512], F32) for i in range(3)])
                    ptr = Ring([P.sb(esC, "pt%d" % i, [128, 512], BF16) for i in range(4)])
                    rdr = Ring([P.sb(esC, "rd%d" % i, [128, 512], F32) for i in range(2)])
                    QKB = (5, 6, 7, 0, 1, 2)
                    PSA = (5, 6, 7, 0)
                    PSB = (1, 2)
                    PSC = (3, 4)
                    SB_ = (0, 1, 2)
                    OB = (3, 4)

                    def stage_P1(u):
                        ps = P.bank(PSA)
                        u["ps"] = ps
                        u["proj"](ps)

                    def stage_P2(u):
                        sq = sqh.next()
                        u["sq"] = sq
                        n = u["n"]
                        P.act(sq[0:96, 0:n], u["ps"][0:96, 0:n], AF.Square)

                    def stage_N1(u):
                        n, sq = u["n"], u["sq"]
                        ps2 = P.bank(PSB)
                        P.mm(ps2[0:96, 0:n], ones[0:96, 0:96], sq[0:96, 0:n])
                        rs = rsh.next()
                        u["rs"] = rs
                        P.act(rs[0:96, 0:n], ps2[0:96, 0:n], AF.Ln, bias=epsc[0:96, EPSI[96]:EPSI[96] + 1])

                    def stage_N2(u):
                        n, ps, out, rs = u["n"], u["ps"], u["out"], u["rs"]
                        P.act(rs[0:96, 0:n], rs[0:96, 0:n], AF.Exp, scale=-0.5)
                        P.stt("dve", out, ps[0:96, 0:n], gsc[0:96, u["gcol"]:u["gcol"] + 1], rs[0:96, 0:n], ALU.mult, ALU.mult)

                    def stage_R(u):
                        rope_pos = u["rope"]
                        if rope_pos is None:
                            return
                        n, out = u["n"], u["out"]
                        ps3 = P.bank(PSC)
                        P.mm(ps3[0:96, 0:n], srot[:], out)
                        t1 = t1r.next()
                        t2 = t2r.next()
                        lo = bass.AP(out.tensor, out.offset + 64 * out.ap[0][0], [[out.ap[0][0], 32]] + [list(x) for x in out.ap[1:]])
                        P.tt("dve", t1[64:96, 0:n], ps3[64:96, 0:n], ropeS[64:96, rope_pos:rope_pos + n], ALU.mult)
                        P.tt("pool", t2[64:96, 0:n], lo, ropeC[64:96, rope_pos:rope_pos + n], ALU.mult)
                        P.tt("pool", lo, t1[64:96, 0:n], t2[64:96, 0:n], ALU.add)

                    def run_units(units):
                        nu = len(units)
                        for st in range(nu + 2):
                            if st < nu:
                                stage_P1(units[st])
                            if 0 <= st - 1 < nu:
                                stage_N1(units[st - 1])
                            if st < nu:
                                stage_P2(units[st])
                            if 0 <= st - 1 < nu:
                                stage_N2(units[st - 1])
                            if 0 <= st - 2 < nu:
                                stage_R(units[st - 2])

                    for si, (t0, L) in enumerate(SEQS):
                        latent = (si == 2)
                        if latent:
                            kvsegs = [(1536, 512, None), (512, 512, 0), (1024, 512, 512)]
                        else:
                            kvsegs = [(t0, 256, None)]
                        Lk = sum(s[1] for s in kvsegs)
                        nkt = Lk // 128
                        qblocks = [(0, 512), (512, 512)] if latent else [(0, 256)]
                        for hg in range(2):
                            units = []
                            for hl in range(4):
                                h = hg * 4 + hl
                                for (q0, n) in qblocks:
                                    def projq(ps, h=h, q0=q0, n=n):
                                        P.mmg(ps[0:96, 0:n], [(wuq[:, kc, h * 96:(h + 1) * 96], cqn[:, kc, t0 + q0:t0 + q0 + n]) for kc in range(3)])
                                    units.append(dict(n=n, proj=projq, gcol=5, out=qT[0:96, hl, q0:q0 + n], rope=(q0 if latent else None)))
                            for hl in range(4):
                                h = hg * 4 + hl
                                ko = 0
                                for (k0, n, rp) in kvsegs:
                                    def projk(ps, h=h, k0=k0, n=n):
                                        P.mm(ps[0:96, 0:n], epl[:], kr_bf[0:32, k0:k0 + n], start=True, stop=False, inc=False, sgc=True)
                                        P.mm(ps[0:64, 0:n], wukv[:, 0, h * 128:h * 128 + 64], ckvn[:, 0, k0:k0 + n], start=False, stop=False, inc=False, sgc=True)
                                        P.mm(ps[0:64, 0:n], wukv[:, 1, h * 128:h * 128 + 64], ckvn[:, 1, k0:k0 + n], start=False, stop=True, sgc=True)
                                    units.append(dict(n=n, proj=projk, gcol=6, out=kT[0:96, hl, ko:ko + n], rope=rp))
                                    ko += n
                            run_units(units)
                            kt = 0
                            for (k0, n, rp) in kvsegs:
                                for i in range(n // 128):
                                    ps = P.bank(QKB)
                                    rhs = [bass.AP(wukv, kc * 1024 + hg * 512 + 64, [[2048, 128], [128, 4], [1, 64]]) for kc in range(2)]
                                    tok = slice(k0 + i * 128, k0 + (i + 1) * 128)
                                    P.mmg(ps[:, 0:256], [(ckvn[:, kc, tok], rhs[kc]) for kc in range(2)])
                                    pv = ps[:, 0:256].rearrange("p (a b d) -> p a b d", a=2, b=2)
                                    P.cp("dve", bass.AP(Vsb, kt * 512, [[12 * 512, 128], [256, 2], [1, 64]]), pv[:, :, 0, :])
                                    P.cp("dve", bass.AP(Vsb, kt * 512 + 128 + 64, [[12 * 512, 128], [256, 2], [1, 64]]), pv[:, :, 1, :])
                                    kt += 1
                            for hl in range(4):
                                chunk = 4 + hg * 2 + hl // 2
                                for (q0, n) in qblocks:
                                    O = P.bank(OB)
                                    pend = None
                                    for kt in range(nkt + 1):
                                        if kt < nkt:
                                            S = P.bank(SB_)
                                            P.mm(S[:, 0:n], kT[0:96, hl, kt * 128:(kt + 1) * 128], qT[0:96, hl, q0:q0 + n])
                                            pt = ptr.next()
                                            P.act(pt[:, 0:n], S[:, 0:n], AF.Exp, scale=float(96.0 ** -0.5))
                                        if pend is not None:
                                            pk, ppt = pend
                                            P.mm(O[:, 0:n], Vsb[:, pk, hl, :], ppt[:, 0:n], start=(pk == 0), stop=(pk == nkt - 1))
                                        pend = (kt, pt) if kt < nkt else None
                                    rd = rdr.next()
                                    if hl % 2 == 0:
                                        P.recip(rd[64:128, 0:n], O[64:128, 0:n])
                                        P.tt("dve", yab[0:64, chunk, t0 + q0:t0 + q0 + n], O[0:64, 0:n], rd[64:128, 0:n], ALU.mult)
                                    else:
                                        P.recip(rd[0:64, 0:n], O[0:64, 0:n])
                                        P.tt("dve", yab[64:128, chunk, t0 + q0:t0 + q0 + n], O[64:128, 0:n], rd[0:64, 0:n], ALU.mult)
                                    ada_piece(pool=(5, 6, 7))
                    dbgdump("qT", qT[:, 0, :], [96, 1024])
                    dbgdump("kT", kT[:, 0, :], [96, 1536])
                dbgdump("yatt", yab[:, 4, :], [128, T])
                dbgdump("F_yab", yab[:].rearrange("p c t -> p (c t)"), [128, 8 * T])

                ada_ensure(12)
                with ExitStack() as esE:
                    wo = P.sb(esE, "wo", [128, 8, 1024], BF16)
                    P.dma("pool", wo[:], wview(ab_w_out), "wo")
                    for b in range(3):
                        for m in range(8):
                            ps = P.bank()
                            P.mmg(ps[:], [(wo[:, kc, m * 128:(m + 1) * 128], yab[:, kc, blk(b)]) for kc in range(8)])
                            residual_update(ps, 0, 2, m, b)
            ada_ensure(48)
            esAda.close()
            dbgdump("mod", mod[:].rearrange("p l c k -> p (l c k)"), [128, 192])
            dbgdump("x_mix0", xT[:, 0, :], [128, T])
            dbgdump("F_x0", xT[:].rearrange("p c t -> p (c t)"), [128, 8 * T])

            ffn(0)
            dbgdump("x_l0", xT[:, 0, :], [128, T])
            dbgdump("F_x1", xT[:].rearrange("p c t -> p (c t)"), [128, 8 * T])

            with ExitStack() as esG:
                giv = wview(gm_w_in)
                wr_t = [P.sb(esG, "gws%d" % i, [128, 8, 512], BF16) for i in range(2)]
                P.dma("pool", wr_t[0][:], giv[:, :, 0:512], "gw0")
                P.dma("pool", wr_t[1][:], giv[:, :, 512:1024], "gw1")
                hT = P.sb(esG, "hT", [128, 8, T], BF16)
                with ExitStack() as es3:
                    xnorm(es3, 1, 0, hT)
                uTb = P.sb(esG, "uTb", [128, 8, T], BF16)
                gT = P.sb(esG, "gT", [128, 8, T], BF16)
                vraw = P.sb(esG, "vraw", [128, 12, 1024], BF16)
                vn = vraw
                wsT = P.sb(esG, "wsT", [128, 8, 128], BF16)
                bsbc = P.sb(esG, "bsbc", [128, 1024], F32)
                gvbc = P.sb(esG, "gvbc", [128, 1024], F32)
                ssq = P.sb(esG, "ssq", [128, 24], F32)
                rsv = P.sb(esG, "rsv", [128, 12], F32)
                P.dma("pool", wsT[:], wsT_d, "gmc2")
                P.dma("sp", bsbc[:], bsbc_d, "gmc")
                P.dma("sp", gvbc[:], gvbc_d, "gmc")
                sqv = Ring([P.sb(esG, "sqv%d" % i, [128, 512], BF16) for i in range(3)])
                tb = Ring([P.sb(esG, "gtb%d" % i, [128, 512], F32) for i in range(3)])
                n = 2
                for pc in range(2):
                    sl = wr_t[pc]
                    for mm_ in range(4):
                        m = pc * 4 + mm_
                        for b in range(3):
                            ps = P.bank()
                            P.mmg(ps[:], [(sl[:, kc, mm_ * 128:(mm_ + 1) * 128], hT[:, kc, blk(b)]) for kc in range(8)])
                            P.cp("act", uTb[:, m, blk(b)], ps[:])
                for half in range(2):
                    P.dma("pool", wr_t[half][:], giv[:, :, 1024 + half * 512:1024 + (half + 1) * 512], "gw%d" % half)

                def v_tile(i):
                    for half in range(2):
                        ps = P.bank()
                        P.mmg(ps[:], [(hT[:, kc, i * 128:(i + 1) * 128], wr_t[half][:, kc, :]) for kc in range(8)])
                        P.cp("act", vraw[:, i, half * 512:(half + 1) * 512], ps[:])
                        sq = sqv.next()
                        P.act(sq[:], ps[:], AF.Square)
                        P.op("dve", lambda g, sq=sq, i=i, half=half: g.reduce_sum(out=ssq[:, 2 * i + half:2 * i + half + 1], in_=sq[:], axis=AX.X),
                             [sq[:]], [ssq[:, 2 * i + half:2 * i + half + 1]])
                    r = rsv[:, i:i + 1]
                    P.tt("dve", r, ssq[:, 2 * i:2 * i + 1], ssq[:, 2 * i + 1:2 * i + 2], ALU.add)
                    P.act(r, r, AF.Ln, bias=epsc[:, EPSI[1024]:EPSI[1024] + 1])
                    P.act(r, r, AF.Exp, scale=-0.5, bias=epsc[:, 4:5])
                    P.stt("dve", vn[:, i, :], vraw[:, i, :], r, gvbc[:], ALU.mult, ALU.mult)

                def sp_tile(i):
                    for gh in range(2):
                        ps = P.bank()
                        for gl in range(4):
                            g = gh * 4 + gl
                            P.mm(ps[:, gl * 128:(gl + 1) * 128], vn[:, i, g * 128:(g + 1) * 128], wsT[:, g, :], inc=(gl == 3))
                        t = tb.next()
                        P.tt("dve", t[:], ps[:], bsbc[:, gh * 512:(gh + 1) * 512], ALU.add)
                        P.tt("pool", gT[:, gh * 4:(gh + 1) * 4, i * 128:(i + 1) * 128],
                             t[:].rearrange("p (g q) -> p g q", g=4), uTb[:, gh * 4:(gh + 1) * 4, i * 128:(i + 1) * 128], ALU.mult)

                for i in range(13):
                    if i < 12:
                        v_tile(i)
                    if i >= 1:
                        sp_tile(i - 1)
                dbgdump("gT", gT[:, 0, :], [128, T])
                with ExitStack() as esE:
                    wo = P.sb(esE, "gwo", [128, 8, 1024], BF16)
                    P.dma("pool", wo[:], wview(gm_w_out), "gwo")
                    for b in range(3):
                        for m in range(8):
                            ps = P.bank()
                            P.mmg(ps[:], [(wo[:, kc, m * 128:(m + 1) * 128], gT[:, kc, blk(b)]) for kc in range(8)])
                            residual_update(ps, 1, 2, m, b)
            dbgdump("x_mix1", xT[:, 0, :], [128, T])
            dbgdump("F_x2", xT[:].rearrange("p c t -> p (c t)"), [128, 8 * T])

            ffn(1)

            with ExitStack() as esO:
                ostg = Ring([P.sb(esO, "ostg%d" % i, [128, 1024], F32) for i in range(2)])
                for i in range(12):
                    st = ostg.next()
                    for half in range(2):
                        ps = P.bank()
                        for j in range(4):
                            kc = half * 4 + j
                            P.tr(ps[:, j * 128:(j + 1) * 128], xT[:, kc, i * 128:(i + 1) * 128], ident[:])
                        P.cp("act" if half == 0 else "dve", st[:, half * 512:(half + 1) * 512], ps[:])
                    P.dma("sp", y_out[i * 128:(i + 1) * 128, :], st[:], "yout")
        except _Stop:
            pass
        for k in ("d_yout", "d_stout", "d_dbg"):
            if k in P.sem:
                nc.sync.wait_ge(P.sem[k], P.dtot[k])
        print("instructions:", P.nins, {e: P.cnt[e] for e in P.cnt})
    return nc


def _host_consts():
    ident = np.eye(128, dtype=np.float32)
    L, GW = 1024, 64
    row = np.repeat(np.arange(L // GW), GW).astype(np.float32)
    col = np.tile(np.arange(GW), L // GW).astype(np.float32)
    inv = (1.0 / (np.float32(10000.0) ** (np.arange(0, 16, 2, dtype=np.float32) / np.float32(16)))).astype(np.float32)
    ang = np.concatenate([row[:, None] * inv, col[:, None] * inv], axis=-1).astype(np.float32)
    cs, sn = np.cos(ang).astype(np.float32), np.sin(ang).astype(np.float32)
    C = np.ones((96, L), np.float32)
    S = np.zeros((96, L), np.float32)
    C[64:80] = cs.T
    C[80:96] = cs.T
    S[64:80] = sn.T
    S[80:96] = sn.T
    rope = np.stack([C, S]).astype(np.float32)
    srot = np.zeros((96, 96), np.float32)
    for i in range(16):
        srot[80 + i, 64 + i] = -1.0
        srot[64 + i, 80 + i] = 1.0
    epl = np.zeros((32, 96), np.float32)
    for i in range(32):
        epl[i, 64 + i] = 1.0
    invcnt = np.zeros((4, T), np.float32)
    for g, w in enumerate((2, 4, 8, 16)):
        for (t0, Ls) in SEQS:
            t = np.arange(Ls)
            lo = np.clip(t - w // 2, 0, Ls)
            hi = np.clip(t + w // 2, 0, Ls)
            invcnt[g, t0:t0 + Ls] = (1.0 / (hi - lo).astype(np.float32)).astype(np.float32)
    return ident, rope, srot, epl, invcnt


def _fm(v):
    return np.ascontiguousarray(np.asarray(v, np.float32).reshape(-1, 128).T)


def make_in_maps(inp):
    ident, rope, srot, epl, invcnt = _host_consts()
    f = lambda k: np.ascontiguousarray(np.asarray(inp[k], dtype=np.float32))
    x_prompt, x_sample = f("x_prompt"), f("x_sample")
    cache_ckv, cache_krope, c, c_ctx = f("cache_ckv"), f("cache_krope"), f("c"), f("c_ctx")
    shared = dict(
        ident=ident, rope=rope, srot=srot, eplace=epl, invcnt=invcnt,
        bsbc=np.ascontiguousarray(np.broadcast_to(f("gm_bs")[0].reshape(1, 1024), (128, 1024))),
        gvbc=np.ascontiguousarray(np.broadcast_to(f("gm_vnorm_g")[0].reshape(1, 1024), (128, 1024))),
        wsT=np.ascontiguousarray(f("gm_ws")[0].transpose(2, 0, 1)),
        ada_w=f("ada_w"), ffn_wg=f("ffn_wg"), ffn_wu=f("ffn_wu"), ffn_wd=f("ffn_wd"),
        ab_w_in=f("ab_w_in")[0], pool_w=f("pool_w")[0], w_uq=f("w_uq")[0], w_ukv=f("w_ukv")[0],
        ab_w_out=f("ab_w_out")[0], gm_w_in=f("gm_w_in")[0], gm_w_out=f("gm_w_out")[0],
    )
    vbase = np.zeros((128, NV), np.float32)
    nmg, nfg, adab = f("norm_mix_g"), f("norm_ffn_g"), f("ada_b")
    for l in range(2):
        vbase[:, l * 8:(l + 1) * 8] = _fm(nmg[l])
        vbase[:, 16 + l * 8:16 + (l + 1) * 8] = _fm(nfg[l])
        vbase[:, 32 + l * 48:32 + (l + 1) * 48] = _fm(adab[l])
    vbase[:, 128:132] = _fm(f("pool_scale")[0])
    vbase[:, 132:135] = _fm(f("q_norm_g")[0])
    vbase[:, 135:137] = _fm(f("kv_norm_g")[0])
    vbase[0:96, 137] = f("qn_g")[0]
    vbase[0:96, 138] = f("kn_g")[0]
    vbase[:, 139:147] = _fm(f("gm_vnorm_g")[0])
    cc = _fm(c_ctx)
    maps = []
    for i in range(8):
        v = vbase.copy()
        v[:, 147:163:2] = cc
        v[:, 148:163:2] = _fm(c[i])
        m = dict(shared)
        m["x_in"] = np.ascontiguousarray(np.concatenate([x_prompt[2 * i], x_prompt[2 * i + 1], x_sample[i]], axis=0))
        m["cckv"] = np.ascontiguousarray(cache_ckv[i, 0])
        m["ckr"] = np.ascontiguousarray(cache_krope[i, 0])
        m["vecs"] = v
        maps.append(m)
    return maps


_NC_CACHE = {}


def kernel(**inputs):
    if "nc" not in _NC_CACHE:
        _NC_CACHE["nc"] = build()
    nc = _NC_CACHE["nc"]
    maps = make_in_maps(inputs)
    res = run_bass_kernel_spmd(nc, maps, core_ids=list(range(8)))
    y_prompt = np.zeros((16, 256, 1024), np.float32)
    y_sample = np.zeros((8, 1024, 1024), np.float32)
    sckv = np.zeros((16, 1, 256, 256), np.float32)
    skr = np.zeros((16, 1, 256, 32), np.float32)
    for i, r in enumerate(res.results):
        y = np.asarray(r["y"], np.float32)
        y_prompt[2 * i] = y[0:256]
        y_prompt[2 * i + 1] = y[256:512]
        y_sample[i] = y[512:1536]
        a = np.asarray(r["sckv"], np.float32)
        b = np.asarray(r["skr"], np.float32)
        sckv[2 * i, 0] = a[0:256]
        sckv[2 * i + 1, 0] = a[256:512]
        skr[2 * i, 0] = b[0:256]
        skr[2 * i + 1, 0] = b[256:512]
    return (y_prompt, y_sample, sckv, skr)
```

```python
import numpy as np
from contextlib import ExitStack
import concourse.bass as bass
import concourse.mybir as mybir
from concourse.bass_utils import run_bass_kernel_spmd

F32 = mybir.dt.float32
BF16 = mybir.dt.bfloat16
AF = mybir.ActivationFunctionType
ALU = mybir.AluOpType
AX = mybir.AxisListType

EPS = 1e-6
T = 1536
NV = 163
SEQS = [(0, 256), (256, 256), (512, 1024)]
PADOFF = [0, 272, 544]
TP = 1584


def _esize(dt):
    return 2 if dt == BF16 else 4


class Prog:
    def __init__(self, nc, es):
        self.nc = nc
        self.E = dict(pe=nc.tensor, act=nc.scalar, dve=nc.vector, pool=nc.gpsimd, sp=nc.sync)
        self.sem = {}
        self.cnt = {}
        self.es = es
        for e in ("pe", "act", "dve", "pool"):
            self.sem["c_" + e] = es.enter_context(nc.semaphore("c_" + e))
            self.cnt[e] = 0
        self.known = {e: {} for e in self.E}
        self.recs = {}
        self.freed = []
        self.dtot = {}
        self.dlast = {}
        self.banks = []
        self.bank_i = {}
        self.nins = 0

    def sb(self, es, name, shape, dt):
        self.uid = getattr(self, "uid", 0) + 1
        name = "%s_%d" % (name, self.uid)
        t = es.enter_context(self.nc.sbuf_tensor(name, list(shape), dt))
        addr = self.nc.lookup_mloc(t).addr
        fsz = int(np.prod(shape[1:]))
        nbytes = fsz * _esize(dt)
        rec = {"w": [], "r": []}
        for (a0, a1, evd) in self.freed:
            if a0 < addr + nbytes and addr < a1:
                for s, v in evd.items():
                    rec["w"].append((0, 128, 0, fsz, s, v))
        self.recs[name] = rec
        es.callback(self._free, name, addr, nbytes)
        return t

    def _free(self, name, addr, nbytes):
        rec = self.recs.pop(name)
        evd = {}
        for x in rec["w"] + rec["r"]:
            if x[5] > evd.get(x[4], 0):
                evd[x[4]] = x[5]
        self.freed.append((addr, addr + nbytes, evd))

    def psum_banks(self, es):
        for i in range(8):
            t = es.enter_context(self.nc.psum_tensor("pb%d" % i, [128, 512], F32))
            self.recs["pb%d" % i] = {"w": [], "r": []}
            self.banks.append(t)

    def bank(self, pool=(0, 1, 2, 3, 4, 5, 6, 7)):
        i = self.bank_i.get(pool, 0)
        self.bank_i[pool] = (i + 1) % len(pool)
        return self.banks[pool[i]]

    def dsem(self, key):
        k = "d_" + key
        if k not in self.sem:
            self.sem[k] = self.es.enter_context(self.nc.semaphore(k))
            self.dtot[k] = 0
        return k

    @staticmethod
    def reg(a):
        sp = str(a.space)
        if "DRAM" in sp:
            return None
        if "PSUM" in sp:
            return (a.tensor.name, 0, 128, 0, 512)
        ap = a.ap
        ps = ap[0][0]
        off = a.offset
        if ps == 0:
            p0, f0 = 0, off
        else:
            p0 = off // ps
            f0 = off - p0 * ps
        ext = 1
        for st, c in ap[1:]:
            ext += (c - 1) * st
        return (a.tensor.name, p0, p0 + ap[0][1], f0, f0 + ext)

    def _deps(self, rr, wr, e=None):
        evs = []
        for r in rr:
            for w in self.recs[r[0]]["w"]:
                if w[0] < r[2] and r[1] < w[1] and w[2] < r[4] and r[3] < w[3]:
                    evs.append((w[4], w[5]))
            if r[0].startswith("pb"):
                for w in self.recs[r[0]]["r"]:
                    if w[4] != "c_" + str(e):
                        evs.append((w[4], w[5]))
        for r in wr:
            rec = self.recs[r[0]]
            for w in rec["w"]:
                if w[0] < r[2] and r[1] < w[1] and w[2] < r[4] and r[3] < w[3]:
                    evs.append((w[4], w[5]))
            for w in rec["r"]:
                if w[0] < r[2] and r[1] < w[1] and w[2] < r[4] and r[3] < w[3]:
                    evs.append((w[4], w[5]))
        return evs

    def _record(self, rr, wr, ev):
        s, v = ev
        for r in rr:
            lst = self.recs[r[0]]["r"]
            key = r[1:]
            for i, x in enumerate(lst):
                if x[4] == s and x[:4] == key:
                    lst[i] = key + (s, v)
                    break
            else:
                lst.append(key + (s, v))
        for r in wr:
            rec = self.recs[r[0]]
            for nm in ("w", "r"):
                rec[nm] = [x for x in rec[nm]
                           if not (r[1] <= x[0] and x[1] <= r[2] and r[3] <= x[2] and x[3] <= r[4])]
            rec["w"].append(r[1:] + (s, v))

    def _waits(self, e, evs):
        need = {}
        k = self.known[e]
        for s, v in evs:
            if k.get(s, 0) >= v:
                continue
            if s in self.dtot:
                v = max(v, self.dtot[s])
            if v > need.get(s, 0):
                need[s] = v
        out = []
        for s, v in need.items():
            if e == "pe" and s == "c_pe":
                continue
            if k.get(s, 0) < v:
                k[s] = v
                out.append((s, v))
        return out

    def op(self, e, fn, reads, writes, inc=True):
        rr = [x for x in (self.reg(a) for a in reads) if x is not None]
        wr = [x for x in (self.reg(a) for a in writes) if x is not None]
        waits = self._waits(e, self._deps(rr, wr, e))
        eng = self.E[e]
        attach = (e != "pe")
        for s, v in (waits[:-1] if attach else waits):
            eng.wait_ge(self.sem[s], v)
        ins = fn(eng)
        if waits and attach:
            ins._wait_ge(self.sem[waits[-1][0]], waits[-1][1])
        if inc:
            ins.then_inc(self.sem["c_" + e], 1)
            self.cnt[e] += 1
            ev = ("c_" + e, self.cnt[e])
        else:
            ev = ("c_" + e, self.cnt[e] + 1)
        self._record(rr, wr, ev)
        self.nins += 1
        return ins

    def dma(self, q, out, in_, key, slow=False):
        rr = [x for x in (self.reg(in_),) if x is not None]
        wr = [x for x in (self.reg(out),) if x is not None]
        waits = self._waits(q, self._deps(rr, wr))
        eng = self.E[q]
        for s, v in waits:
            eng.wait_ge(self.sem[s], v)
        k = self.dsem(key)
        side = out if wr else in_
        nm = side.tensor.name
        last = self.dlast.get(k)
        if last is not None and last != nm and self.known[q].get(k, 0) < self.dtot[k]:
            eng.wait_ge(self.sem[k], self.dtot[k])
            self.known[q][k] = self.dtot[k]
        self.dlast[k] = nm
        if slow:
            ins = eng.dma_start(out=out, in_=in_, allow_slow_non_contiguous=True)
        else:
            ins = eng.dma_start(out=out, in_=in_)
        ins.then_inc(self.sem[k], 16)
        self.dtot[k] += 16
        self._record(rr, wr, (k, self.dtot[k]))
        return ins

    def mm(self, out, lhsT, rhs, start=True, stop=True, inc=True, sgc=False):
        if sgc:
            return self.op("pe", lambda g: g.matmul(out, lhsT, rhs, start=start, stop=stop, skip_group_check=True),
                           [lhsT, rhs], [out], inc=inc)
        return self.op("pe", lambda g: g.matmul(out, lhsT, rhs, start=start, stop=stop),
                       [lhsT, rhs], [out], inc=inc)

    def mmg(self, out, pairs):
        n = len(pairs)
        for i, (l, r) in enumerate(pairs):
            self.mm(out, l, r, start=(i == 0), stop=(i == n - 1), inc=(i == n - 1))

    def tr(self, out, in_, ident):
        return self.op("pe", lambda g: g.transpose(out, in_, ident), [in_, ident], [out])

    def act(self, out, in_, func, scale=1.0, bias=0.0):
        reads = [in_]
        if not isinstance(scale, (int, float)):
            reads.append(scale)
        if not isinstance(bias, (int, float)):
            reads.append(bias)
        return self.op("act", lambda g: g.activation(out=out, in_=in_, func=func, bias=bias, scale=scale),
                       reads, [out])

    def tt(self, e, out, in0, in1, op):
        return self.op(e, lambda g: g.tensor_tensor(out=out, in0=in0, in1=in1, op=op), [in0, in1], [out])

    def ts(self, e, out, in0, s1, s2, op0, op1=None):
        reads = [in0]
        for s in (s1, s2):
            if s is not None and not isinstance(s, (int, float)):
                reads.append(s)
        if op1 is None:
            return self.op(e, lambda g: g.tensor_scalar(out=out, in0=in0, scalar1=s1, scalar2=None, op0=op0),
                           reads, [out])
        return self.op(e, lambda g: g.tensor_scalar(out=out, in0=in0, scalar1=s1, scalar2=s2, op0=op0, op1=op1),
                       reads, [out])

    def stt(self, e, out, in0, sc, in1, op0, op1):
        reads = [in0, in1]
        if not isinstance(sc, (int, float)):
            reads.append(sc)
        return self.op(e, lambda g: g.scalar_tensor_tensor(out=out, in0=in0, scalar=sc, in1=in1, op0=op0, op1=op1),
                       reads, [out])

    def cp(self, e, out, in_):
        if e == "act":
            return self.op("act", lambda g: g.copy(out=out, in_=in_), [in_], [out])
        return self.op(e, lambda g: g.tensor_copy(out=out, in_=in_), [in_], [out])

    def memset(self, e, out, val):
        return self.op(e, lambda g: g.memset(out, val), [], [out])

    def recip(self, out, in_):
        return self.op("dve", lambda g: g.reciprocal(out=out, in_=in_), [in_], [out])


class Ring:
    def __init__(self, tiles):
        self.t = tiles
        self.i = 0

    def next(self):
        t = self.t[self.i]
        self.i = (self.i + 1) % len(self.t)
        return t


def wview(ap2d):
    return ap2d.rearrange("(kc p) n -> p kc n", p=128)


class _Stop(Exception):
    pass


def build(dbg=(), stop_after=None):
    nc = bass.Bass("TRN2", target_bir_lowering=False)

    def din(name, shape):
        return nc.dram_tensor(name, list(shape), F32, kind="ExternalInput").ap()

    def dout(name, shape):
        return nc.dram_tensor(name, list(shape), F32, kind="ExternalOutput").ap()

    x_in = din("x_in", [T, 1024])
    cckv = din("cckv", [512, 256])
    ckr = din("ckr", [512, 32])
    vecs_d = din("vecs", [128, NV])
    ident_d = din("ident", [128, 128])
    rope_d = din("rope", [2, 96, 1024])
    srot_d = din("srot", [96, 96])
    eplace_d = din("eplace", [32, 96])
    invcnt_d = din("invcnt", [4, T])
    bsbc_d = din("bsbc", [128, 1024])
    gvbc_d = din("gvbc", [128, 1024])
    wsT_d = din("wsT", [128, 8, 128])
    ada_w = din("ada_w", [2, 1024, 6144])
    ffn_wg = din("ffn_wg", [2, 1024, 2816])
    ffn_wu = din("ffn_wu", [2, 1024, 2816])
    ffn_wd = din("ffn_wd", [2, 2816, 1024])
    ab_w_in = din("ab_w_in", [1024, 1184])
    pool_w = din("pool_w", [4, 128, 128])
    w_uq = din("w_uq", [384, 768])
    w_ukv = din("w_ukv", [256, 1024])
    ab_w_out = din("ab_w_out", [1024, 1024])
    gm_w_in = din("gm_w_in", [1024, 2048])
    gm_w_out = din("gm_w_out", [1024, 1024])

    y_out = dout("y", [T, 1024])
    sckv_out = dout("sckv", [512, 256])
    skr_out = dout("skr", [512, 32])
    dbg_out = {}

    with ExitStack() as es:
        P = Prog(nc, es)
        P.psum_banks(es)

        def dbgdump(name, ap, shape):
            if name in dbg:
                d = nc.dram_tensor("dbg_" + name, list(shape), ap.dtype, kind="ExternalOutput").ap()
                dbg_out[name] = d
                P.dma("sp", d, ap, "dbg")
            if name == stop_after:
                raise _Stop()

        try:
            xT = P.sb(es, "xT", [128, 8, T], F32)
            vecs = P.sb(es, "vecs", [128, NV], F32)
            ident = P.sb(es, "ident", [128, 128], F32)
            ones = P.sb(es, "ones", [128, 128], BF16)
            mod = P.sb(es, "mod", [128, 2, 48, 2], F32)
            AA = P.sb(es, "AA", [128, 2, 2, 8, 2], F32)
            gsc = P.sb(es, "gsc", [128, 8], F32)
            epsc = P.sb(es, "epsc", [128, 5], F32)
            EPSI = {1024: 0, 384: 1, 256: 2, 96: 3}
            for n_, i_ in EPSI.items():
                P.memset("dve", epsc[:, i_:i_ + 1], float(n_) * EPS)
            P.memset("dve", epsc[:, 4:5], float(np.log(32.0)))

            def rstd(out, in_, n_, np_=128):
                P.act(out, in_, AF.Ln, bias=epsc[0:np_, EPSI[n_]:EPSI[n_] + 1])
                P.act(out, out, AF.Exp, scale=-0.5)

            P.dma("sp", vecs[:], vecs_d, "const")
            P.dma("sp", ident[:], ident_d, "const")
            P.memset("dve", ones[:], 1.0)
            P.ts("dve", gsc[:, 0:3], vecs[:, 132:135], float(np.sqrt(384.0)), None, ALU.mult)
            P.ts("dve", gsc[:, 3:5], vecs[:, 135:137], 16.0, None, ALU.mult)
            P.ts("dve", gsc[:, 5:7], vecs[:, 137:139], float(np.sqrt(96.0)), None, ALU.mult)

            def blk(b):
                return slice(b * 512, (b + 1) * 512)

            def load_transposed(src, ntile, F, dst3, evac_engs=("act", "dve"), nring=2, per_tile=None):
                nfc = (F + 127) // 128
                with ExitStack() as es2:
                    ring = Ring([P.sb(es2, "ldt%d" % i, [128, F], F32) for i in range(nring)])
                    k = 0
                    for i in range(ntile):
                        st = ring.next()
                        P.dma("sp", st[:], src[i * 128:(i + 1) * 128, :], "ldt%d" % (i % nring))
                        for f0 in range(0, nfc, 4):
                            nb = min(4, nfc - f0)
                            fp = min(128, F - f0 * 128)
                            ps = P.bank()
                            for j in range(nb):
                                fc = f0 + j
                                P.tr(ps[0:fp, j * 128:(j + 1) * 128], st[:, fc * 128:fc * 128 + fp], ident[:])
                            P.cp(evac_engs[k % 2], dst3(f0, nb, i, fp),
                                 ps[0:fp, 0:nb * 128].rearrange("p (a b) -> p a b", b=128))
                            k += 1
                        if per_tile is not None:
                            per_tile(i)

            scT = P.sb(es, "scT", [128, 16], BF16)
            P.act(scT[:], vecs[:, 147:163], AF.Silu)
            esAda = ExitStack()
            ada_slots = [P.sb(esAda, "adaw%d" % i, [128, 8, 256], BF16) for i in range(3)]
            ada_state = {"done": 0}
            aa_done = set()

            def ada_piece(pool=None):
                k = ada_state["done"]
                if k >= 48:
                    return
                ada_state["done"] = k + 1
                l, c0 = divmod(2 * k, 48)
                sl = ada_slots[k % 3]
                P.dma("pool", sl[:], wview(ada_w[l])[:, :, c0 * 128:(c0 + 2) * 128], "adaw%d" % (k % 3))
                psb = P.bank(pool) if pool else P.bank()
                for m in range(2):
                    P.mmg(psb[:, 2 * m:2 * m + 2],
                          [(sl[:, kc, m * 128:(m + 1) * 128], scT[:, 2 * kc:2 * kc + 2]) for kc in range(8)])
                bias_bc = bass.AP(vecs, 32 + l * 48 + c0, [[NV, 128], [1, 2], [0, 2]])
                P.tt("dve", mod[:, l, c0:c0 + 2, :], psb[:, 0:4].rearrange("p (c k) -> p c k", k=2), bias_bc, ALU.add)

            def ada_ensure(n):
                while ada_state["done"] < n:
                    ada_piece()

            def ensure_AA(l, w):
                if (l, w) in aa_done:
                    return
                aa_done.add((l, w))
                term = 1 if w == 0 else 4
                ada_ensure(l * 24 + (term * 8 + 8) // 2)
                P.ts("dve", AA[:, l, w], mod[:, l, term * 8:(term + 1) * 8, :], 1.0, 32.0, ALU.add, ALU.mult)
                gcol = (0 if w == 0 else 16) + l * 8
                g_bc = bass.AP(vecs, gcol, [[NV, 128], [1, 8], [0, 2]])
                P.tt("dve", AA[:, l, w], AA[:, l, w], g_bc, ALU.mult)

            load_transposed(x_in, 12, 1024, lambda f0, nb, i, fp: xT[:, f0:f0 + nb, i * 128:(i + 1) * 128], nring=4,
                            per_tile=lambda i: (ada_piece() if ada_state["done"] < 8 else None))
            ensure_AA(0, 0)
            dbgdump("xT0", xT[:, 0, :], [128, T])

            def mscal(l, term, kc, cond):
                return mod[:, l, term * 8 + kc, cond:cond + 1]

            def xnorm(es2, l, w, hT, small=False):
                ensure_AA(l, w)
                sqr = Ring([P.sb(es2, "xsq%d" % i, [128, 512], BF16) for i in range(2 if small else 4)])
                rsr = Ring([P.sb(es2, "xrs%d" % i, [128, 512], F32) for i in range(1 if small else 3)])
                tmpr = Ring([P.sb(es2, "xtmp%d" % i, [128, 512], F32) for i in range(2 if small else 4)])
                shterm = 0 if w == 0 else 3
                for b in range(3):
                    cond = 0 if b == 0 else 1
                    ps = P.bank()
                    rs = rsr.next()
                    for kc in range(8):
                        sq = sqr.next()
                        if kc % 2 == 0:
                            P.act(sq[:], xT[:, kc, blk(b)], AF.Square)
                        else:
                            P.tt("dve", sq[:], xT[:, kc, blk(b)], xT[:, kc, blk(b)], ALU.mult)
                        P.mm(ps[:], ones[:], sq[:], start=(kc == 0), stop=(kc == 7))
                    rstd(rs[:], ps[:], 1024)
                    for kc in range(8):
                        tmp = tmpr.next()
                        P.tt("dve", tmp[:], xT[:, kc, blk(b)], rs[:], ALU.mult)
                        if kc % 2 == 0:
                            P.act(hT[:, kc, blk(b)], tmp[:], AF.Identity,
                                  scale=AA[:, l, w, kc, cond:cond + 1], bias=mscal(l, shterm, kc, cond))
                        else:
                            P.ts("pool", hT[:, kc, blk(b)], tmp[:], AA[:, l, w, kc, cond:cond + 1],
                                 mscal(l, shterm, kc, cond), ALU.mult, ALU.add)

            def residual_update(ps, l, term, m, b):
                cond = 0 if b == 0 else 1
                P.stt("dve", xT[:, m, blk(b)], ps[:], mscal(l, term, m, cond), xT[:, m, blk(b)], ALU.mult, ALU.add)

            def ffn(l):
                ada_ensure(l * 24 + 24)
                with ExitStack() as es2:
                    wgs = [P.sb(es2, "wgs%d" % i, [128, 8, 256], BF16) for i in range(3)]
                    wus = [P.sb(es2, "wus%d" % i, [128, 8, 256], BF16) for i in range(3)]
                    wgv = wview(ffn_wg[l])
                    wuv = wview(ffn_wu[l])
                    wdv = wview(ffn_wd[l])
                    issued = {}

                    def issue(k):
                        if k in issued or k > 10:
                            return
                        jp = 2 * k
                        wg_s, wu_s = wgs[k % 3], wus[k % 3]
                        P.dma("pool", wg_s[:], wgv[:, :, jp * 128:(jp + 2) * 128], "wg%d" % (k % 3))
                        P.dma("pool", wu_s[:], wuv[:, :, jp * 128:(jp + 2) * 128], "wu%d" % (k % 3))
                        issued[k] = (wg_s, wu_s)

                    issue(0)
                    issue(1)
                    hT = P.sb(es2, "hT", [128, 8, T], BF16)
                    with ExitStack() as es3:
                        xnorm(es3, l, 1, hT)
                    groups = [(0, 6), (6, 12), (12, 18), (18, 22)]
                    aparts = [P.sb(es2, "apart%d" % i, [128, 6, T], BF16) for i in range(2)]
                    wdb = [P.sb(es2, "wdb%d" % i, [128, 6, 1024], BF16) for i in range(2)]
                    sgr = Ring([P.sb(es2, "sg%d" % i, [128, 512], F32) for i in range(3)])

                    def up(gi, after_first=None):
                        j0, j1 = groups[gi]
                        ap_ = aparts[gi % 2]
                        for jp in range(j0, j1, 2):
                            k = jp // 2
                            issue(k)
                            issue(k + 1)
                            issue(k + 2)
                            wg_s, wu_s = issued[k]
                            if jp == j0 and after_first is not None:
                                after_first()
                            for jj in range(2):
                                j = jp + jj
                                for b in range(3):
                                    psg = P.bank()
                                    psu = P.bank()
                                    P.mmg(psg[:], [(wg_s[:, kc, jj * 128:(jj + 1) * 128], hT[:, kc, blk(b)]) for kc in range(8)])
                                    P.mmg(psu[:], [(wu_s[:, kc, jj * 128:(jj + 1) * 128], hT[:, kc, blk(b)]) for kc in range(8)])
                                    sg = sgr.next()
                                    P.act(sg[:], psg[:], AF.Silu)
                                    P.tt("dve", ap_[:, j - j0, blk(b)], psu[:], sg[:], ALU.mult)

                    def load_wd(gi):
                        j0, j1 = groups[gi]
                        wb = wdb[gi % 2]
                        for jp in range(j0, j1, 2):
                            P.dma("pool", wb[:, jp - j0:jp - j0 + 2, :], wdv[:, jp:jp + 2, :], "wd%d" % (gi % 2))

                    def down(gi):
                        j0, j1 = groups[gi]
                        ap_ = aparts[gi % 2]
                        wb = wdb[gi % 2]
                        for b in range(3):
                            for m in range(8):
                                ps = P.bank()
                                P.mmg(ps[:], [(wb[:, jj, m * 128:(m + 1) * 128], ap_[:, jj, blk(b)]) for jj in range(j1 - j0)])
                                residual_update(ps, l, 5, m, b)

                    up(0, lambda: load_wd(0))
                    up(1, lambda: load_wd(1))
                    down(0)
                    up(2, lambda: load_wd(2))
                    down(1)
                    up(3, lambda: load_wd(3))
                    down(2)
                    down(3)

            with ExitStack() as esL0:
                cqn = P.sb(esL0, "cqn", [128, 3, T], BF16)
                ckvn = P.sb(esL0, "ckvn", [128, 2, 2048], BF16)
                kr_bf = P.sb(esL0, "kr_bf", [32, 2048], BF16)
                yab = P.sb(esL0, "yab", [128, 8, T], BF16)

                load_transposed(cckv, 4, 256, lambda f0, nb, i, fp: ckvn[:, f0:f0 + nb, 1536 + i * 128:1536 + (i + 1) * 128])
                load_transposed(ckr, 4, 32, lambda f0, nb, i, fp: kr_bf[0:32, 1536 + i * 128:1536 + (i + 1) * 128].rearrange("p (a b) -> p a b", a=1))

                with ExitStack() as esB:
                    uT = P.sb(esB, "uT", [128, 4, TP], F32)
                    P.memset("dve", uT[:], 0.0)
                    with ExitStack() as esA:
                        w_in = P.sb(esA, "w_in", [128, 8, 1184], BF16)
                        wiv = wview(ab_w_in)
                        P.dma("pool", w_in[:, 0:4, :], wiv[:, 0:4, :], "w_in")
                        P.dma("pool", w_in[:, 4:8, :], wiv[:, 4:8, :], "w_in")
                        hT = P.sb(esA, "hT", [128, 8, T], BF16)
                        with ExitStack() as es3:
                            xnorm(es3, 0, 0, hT, small=True)
                        dbgdump("h0", hT[:, 0, :], [128, T])
                        dbgdump("F_h0", hT[:].rearrange("p c t -> p (c t)"), [128, 8 * T])
                        cqf = P.sb(esA, "cqf", [128, 3, 512], F32)
                        sqb = Ring([P.sb(esA, "sqb%d" % i, [128, 512], BF16) for i in range(3)])
                        rsb = P.sb(esA, "rsb", [128, 512], F32)
                        ckvf = P.sb(esA, "ckvf", [128, 2, 512], F32)
                        krf = P.sb(esA, "krf", [32, 512], F32)
                        stg = Ring([P.sb(esA, "stg%d" % i, [128, 288], F32) for i in range(2)])

                        def proj(m, b, M=128):
                            ps = P.bank()
                            P.mmg(ps[0:M, :], [(w_in[:, kc, m * 128:m * 128 + M], hT[:, kc, blk(b)]) for kc in range(8)])
                            return ps

                        for b in range(3):
                            for m in range(4):
                                ps = proj(m, b)
                                if b == 0:
                                    for s in range(2):
                                        P.cp("act", uT[:, m, PADOFF[s] + 8:PADOFF[s] + 8 + 256], ps[:, s * 256:(s + 1) * 256])
                                else:
                                    o = PADOFF[2] + 8 + (b - 1) * 512
                                    P.cp("act", uT[:, m, o:o + 512], ps[:])
                                ada_piece()
                            pss = P.bank()
                            for j in range(3):
                                ps = proj(4 + j, b)
                                P.cp("dve", cqf[:, j, :], ps[:])
                                sq = sqb.next()
                                P.act(sq[:], ps[:], AF.Square)
                                P.mm(pss[:], ones[:], sq[:], start=(j == 0), stop=(j == 2))
                            rstd(rsb[:], pss[:], 384)
                            for j in range(3):
                                P.stt("dve", cqn[:, j, blk(b)], cqf[:, j, :], gsc[:, j:j + 1], rsb[:], ALU.mult, ALU.mult)
                            ada_piece()
                            pss = P.bank()
                            for j in range(2):
                                ps = proj(7 + j, b)
                                P.cp("dve", ckvf[:, j, :], ps[:])
                                sq = sqb.next()
                                P.act(sq[:], ps[:], AF.Square)
                                P.mm(pss[:], ones[:], sq[:], start=(j == 0), stop=(j == 1))
                            rstd(rsb[:], pss[:], 256)
                            for j in range(2):
                                P.stt("dve", ckvn[:, j, blk(b)], ckvf[:, j, :], gsc[:, 3 + j:4 + j], rsb[:], ALU.mult, ALU.mult)
                                if b == 0:
                                    P.stt("dve", ckvf[:, j, :], ckvf[:, j, :], gsc[:, 3 + j:4 + j], rsb[:], ALU.mult, ALU.mult)
                            ada_piece()
                            ps = proj(9, b, M=32)
                            P.cp("act", kr_bf[0:32, blk(b)], ps[0:32, :])
                            if b == 0:
                                P.cp("dve", krf[:], ps[0:32, :])
                                for i in range(4):
                                    pst = P.bank()
                                    for j in range(2):
                                        P.tr(pst[:, j * 128:(j + 1) * 128], ckvf[:, j, i * 128:(i + 1) * 128], ident[:])
                                    P.tr(pst[:, 256:288], krf[0:32, i * 128:(i + 1) * 128], ident[0:32, 0:32])
                                    st = stg.next()
                                    P.cp("act", st[:], pst[:, 0:288])
                                    P.dma("sp", sckv_out[i * 128:(i + 1) * 128, :], st[:, 0:256], "stout")
                                    P.dma("sp", skr_out[i * 128:(i + 1) * 128, :], st[:, 256:288], "stout")
                    dbgdump("cqn", cqn[:, 0, :], [128, T])
                    dbgdump("F_cqn", cqn[:].rearrange("p c t -> p (c t)"), [128, 3 * T])
                    dbgdump("F_ckvn", ckvn[:].rearrange("p c t -> p (c t)"), [128, 2 * 2048])
                    dbgdump("F_kr", kr_bf[:], [32, 2048])
                    dbgdump("F_uT", uT[:].rearrange("p c t -> p (c t)"), [128, 4 * TP])
                    dbgdump("ckvn", ckvn[:, 0, :], [128, 2048])
                    dbgdump("uT", uT[:, 0, :], [128, TP])

                    with ExitStack() as esP:
                        S2 = P.sb(esP, "S2", [128, TP], F32)
                        S4 = P.sb(esP, "S4", [128, TP], F32)
                        S8 = P.sb(esP, "S8", [128, TP], F32)
                        S16 = P.sb(esP, "S16", [128, TP], F32)
                        icr = Ring([P.sb(esP, "ic%d" % i, [128, T], F32) for i in range(2)])
                        pbf = P.sb(esP, "pbf", [128, 4, T], BF16)
                        pw = P.sb(esP, "pw", [128, 4, 128], BF16)
                        P.dma("pool", pw[:], pool_w.rearrange("g c d -> c g d"), "pw")
                        for g in range(4):
                            e = "dve" if g % 2 == 0 else "pool"
                            U = uT[:, g, :]
                            P.tt(e, S2[:, 1:TP], U[:, 0:TP - 1], U[:, 1:TP], ALU.add)
                            S = S2
                            if g >= 1:
                                P.tt(e, S4[:, 2:TP - 1], S2[:, 1:TP - 2], S2[:, 3:TP], ALU.add)
                                S = S4
                            if g >= 2:
                                P.tt(e, S8[:, 4:TP - 3], S4[:, 2:TP - 5], S4[:, 6:TP - 1], ALU.add)
                                S = S8
                            if g >= 3:
                                P.tt(e, S16[:, 8:TP - 7], S8[:, 4:TP - 11], S8[:, 12:TP - 3], ALU.add)
                                S = S16
                            ic = icr.next()
                            P.dma("sp", ic[:], bass.AP(invcnt_d.tensor, g * T, [[0, 128], [1, T]]), "ic%d" % (g % 2))
                            for s, (t0, L) in enumerate(SEQS):
                                o = PADOFF[s] + 8
                                P.tt(e, S[:, o:o + L], S[:, o:o + L], ic[:, t0:t0 + L], ALU.mult)
                                P.tt(e, pbf[:, g, t0:t0 + L], S[:, o:o + L], U[:, o:o + L], ALU.subtract)
                            for b in range(3):
                                ps = P.bank()
                                P.mm(ps[:], pw[:, g, :], pbf[:, g, blk(b)])
                                P.act(yab[:, g, blk(b)], ps[:], AF.Identity, scale=vecs[:, 128 + g:129 + g])
                        dbgdump("pbf", pbf[:, 3, :], [128, T])
                dbgdump("ypool", yab[:, 0, :], [128, T])

                with ExitStack() as esC:
                    wuq = P.sb(esC, "wuq", [128, 3, 768], BF16)
                    wukv = P.sb(esC, "wukv", [128, 2, 1024], BF16)
                    P.dma("pool", wuq[:], wview(w_uq), "wuq")
                    P.dma("pool", wukv[:], wview(w_ukv), "wukv")
                    ropeC = P.sb(esC, "ropeC", [96, 1024], F32)
                    ropeS = P.sb(esC, "ropeS", [96, 1024], F32)
                    P.dma("sp", ropeC[:], rope_d[0], "rope")
                    P.dma("sp", ropeS[:], rope_d[1], "rope")
                    srot_f = P.sb(esC, "srot_f", [96, 96], F32)
                    srot = P.sb(esC, "srot", [96, 96], BF16)
                    epl_f = P.sb(esC, "epl_f", [32, 96], F32)
                    epl = P.sb(esC, "epl", [32, 96], BF16)
                    P.dma("sp", srot_f[:], srot_d, "rope")
                    P.dma("sp", epl_f[:], eplace_d, "rope")
                    P.cp("dve", srot[:], srot_f[:])
                    P.cp("dve", epl[:], epl_f[:])
                    qT = P.sb(esC, "qT", [96, 4, 1024], BF16)
                    kT = P.sb(esC, "kT", [96, 4, 1536], BF16)
                    Vsb = P.sb(esC, "Vsb", [128, 12, 4, 128], BF16)
                    for hl in range(4):
                        if hl % 2 == 0:
                            P.memset("pool", Vsb[:, :, hl, 64:128], 1.0)
                        else:
                            P.memset("pool", Vsb[:, :, hl, 0:64], 1.0)
                    sqh = Ring([P.sb(esC, "sqh%d" % i, [96, 512], BF16) for i in range(4)])
                    rsh = Ring([P.sb(esC, "rsh%d" % i, [96, 512], F32) for i in range(4)])
                    t1r = Ring([P.sb(esC, "t1r%d" % i, [96, 512], F32) for i in range(3)])
                    t2r = Ring([P.sb(esC, "t2r%d" % i, [96, 512], F32) for i in range(3)])
                    ptr = Ring([P.sb(esC, "pt%d" % i, [128, 512], BF16) for i in range(4)])
                    rdr = Ring([P.sb(esC, "rd%d" % i, [128, 512], F32) for i in range(2)])
                    QKB = (5, 6, 7, 0, 1, 2)
                    PSA = (5, 6, 7, 0)
                    PSB = (1, 2)
                    PSC = (3, 4)
                    SB_ = (0, 1, 2)
                    OB = (3, 4)

                    def stage_P1(u):
                        ps = P.bank(PSA)
                        u["ps"] = ps
                        u["proj"](ps)

                    def stage_P2(u):
                        sq = sqh.next()
                        u["sq"] = sq
                        n = u["n"]
                        P.act(sq[0:96, 0:n], u["ps"][0:96, 0:n], AF.Square)

                    def stage_N1(u):
                        n, sq = u["n"], u["sq"]
                        ps2 = P.bank(PSB)
                        P.mm(ps2[0:96, 0:n], ones[0:96, 0:96], sq[0:96, 0:n])
                        rs = rsh.next()
                        u["rs"] = rs
                        P.act(rs[0:96, 0:n], ps2[0:96, 0:n], AF.Ln, bias=epsc[0:96, EPSI[96]:EPSI[96] + 1])

                    def stage_N2(u):
                        n, ps, out, rs = u["n"], u["ps"], u["out"], u["rs"]
                        P.act(rs[0:96, 0:n], rs[0:96, 0:n], AF.Exp, scale=-0.5)
                        P.stt("dve", out, ps[0:96, 0:n], gsc[0:96, u["gcol"]:u["gcol"] + 1], rs[0:96, 0:n], ALU.mult, ALU.mult)

                    def stage_R(u):
                        rope_pos = u["rope"]
                        if rope_pos is None:
                            return
                        n, out = u["n"], u["out"]
                        ps3 = P.bank(PSC)
                        P.mm(ps3[0:96, 0:n], srot[:], out)
                        t1 = t1r.next()
                        t2 = t2r.next()
                        lo = bass.AP(out.tensor, out.offset + 64 * out.ap[0][0], [[out.ap[0][0], 32]] + [list(x) for x in out.ap[1:]])
                        P.tt("dve", t1[64:96, 0:n], ps3[64:96, 0:n], ropeS[64:96, rope_pos:rope_pos + n], ALU.mult)
                        P.tt("pool", t2[64:96, 0:n], lo, ropeC[64:96, rope_pos:rope_pos + n], ALU.mult)
                        P.tt("pool", lo, t1[64:96, 0:n], t2[64:96, 0:n], ALU.add)

                    def run_units(units):
                        nu = len(units)
                        for st in range(nu + 2):
                            if st < nu:
                                stage_P1(units[st])
                            if 0 <= st - 1 < nu:
                                stage_N1(units[st - 1])
                            if st < nu:
                                stage_P2(units[st])
                            if 0 <= st - 1 < nu:
                                stage_N2(units[st - 1])
                            if 0 <= st - 2 < nu:
                                stage_R(units[st - 2])

                    for si, (t0, L) in enumerate(SEQS):
                        latent = (si == 2)
                        if latent:
                            kvsegs = [(1536, 512, None), (512, 512, 0), (1024, 512, 512)]
                        else:
                            kvsegs = [(t0, 256, None)]
                        Lk = sum(s[1] for s in kvsegs)
                        nkt = Lk // 128
                        qblocks = [(0, 512), (512, 512)] if latent else [(0, 256)]
                        for hg in range(2):
                            units = []
                            for hl in range(4):
                                h = hg * 4 + hl
                                for (q0, n) in qblocks:
                                    def projq(ps, h=h, q0=q0, n=n):
                                        P.mmg(ps[0:96, 0:n], [(wuq[:, kc, h * 96:(h + 1) * 96], cqn[:, kc, t0 + q0:t0 + q0 + n]) for kc in range(3)])
                                    units.append(dict(n=n, proj=projq, gcol=5, out=qT[0:96, hl, q0:q0 + n], rope=(q0 if latent else None)))
                            for hl in range(4):
                                h = hg * 4 + hl
                                ko = 0
                                for (k0, n, rp) in kvsegs:
                                    def projk(ps, h=h, k0=k0, n=n):
                                        P.mm(ps[0:96, 0:n], epl[:], kr_bf[0:32, k0:k0 + n], start=True, stop=False, inc=False, sgc=True)
                                        P.mm(ps[0:64, 0:n], wukv[:, 0, h * 128:h * 128 + 64], ckvn[:, 0, k0:k0 + n], start=False, stop=False, inc=False, sgc=True)
                                        P.mm(ps[0:64, 0:n], wukv[:, 1, h * 128:h * 128 + 64], ckvn[:, 1, k0:k0 + n], start=False, stop=True, sgc=True)
                                    units.append(dict(n=n, proj=projk, gcol=6, out=kT[0:96, hl, ko:ko + n], rope=rp))
                                    ko += n
                            run_units(units)
                            kt = 0
                            for (k0, n, rp) in kvsegs:
                                for i in range(n // 128):
                                    ps = P.bank(QKB)
                                    rhs = [bass.AP(wukv, kc * 1024 + hg * 512 + 64, [[2048, 128], [128, 4], [1, 64]]) for kc in range(2)]
                                    tok = slice(k0 + i * 128, k0 + (i + 1) * 128)
                                    P.mmg(ps[:, 0:256], [(ckvn[:, kc, tok], rhs[kc]) for kc in range(2)])
                                    pv = ps[:, 0:256].rearrange("p (a b d) -> p a b d", a=2, b=2)
                                    P.cp("dve", bass.AP(Vsb, kt * 512, [[12 * 512, 128], [256, 2], [1, 64]]), pv[:, :, 0, :])
                                    P.cp("dve", bass.AP(Vsb, kt * 512 + 128 + 64, [[12 * 512, 128], [256, 2], [1, 64]]), pv[:, :, 1, :])
                                    kt += 1
                            for hl in range(4):
                                chunk = 4 + hg * 2 + hl // 2
                                for (q0, n) in qblocks:
                                    O = P.bank(OB)
                                    pend = None
                                    for kt in range(nkt + 1):
                                        if kt < nkt:
                                            S = P.bank(SB_)
                                            P.mm(S[:, 0:n], kT[0:96, hl, kt * 128:(kt + 1) * 128], qT[0:96, hl, q0:q0 + n])
                                            pt = ptr.next()
                                            P.act(pt[:, 0:n], S[:, 0:n], AF.Exp, scale=float(96.0 ** -0.5))
                                        if pend is not None:
                                            pk, ppt = pend
                                            P.mm(O[:, 0:n], Vsb[:, pk, hl, :], ppt[:, 0:n], start=(pk == 0), stop=(pk == nkt - 1))
                                        pend = (kt, pt) if kt < nkt else None
                                    rd = rdr.next()
                                    if hl % 2 == 0:
                                        P.recip(rd[64:128, 0:n], O[64:128, 0:n])
                                        P.tt("dve", yab[0:64, chunk, t0 + q0:t0 + q0 + n], O[0:64, 0:n], rd[64:128, 0:n], ALU.mult)
                                    else:
                                        P.recip(rd[0:64, 0:n], O[0:64, 0:n])
                                        P.tt("dve", yab[64:128, chunk, t0 + q0:t0 + q0 + n], O[64:128, 0:n], rd[0:64, 0:n], ALU.mult)
                                    ada_piece(pool=(5, 6, 7))
                    dbgdump("qT", qT[:, 0, :], [96, 1024])
                    dbgdump("kT", kT[:, 0, :], [96, 1536])
                dbgdump("yatt", yab[:, 4, :], [128, T])
                dbgdump("F_yab", yab[:].rearrange("p c t -> p (c t)"), [128, 8 * T])

                ada_ensure(12)
                with ExitStack() as esE:
                    wo = P.sb(esE, "wo", [128, 8, 1024], BF16)
                    P.dma("pool", wo[:], wview(ab_w_out), "wo")
                    for b in range(3):
                        for m in range(8):
                            ps = P.bank()
                            P.mmg(ps[:], [(wo[:, kc, m * 128:(m + 1) * 128], yab[:, kc, blk(b)]) for kc in range(8)])
                            residual_update(ps, 0, 2, m, b)
            ada_ensure(48)
            esAda.close()
            dbgdump("mod", mod[:].rearrange("p l c k -> p (l c k)"), [128, 192])
            dbgdump("x_mix0", xT[:, 0, :], [128, T])
            dbgdump("F_x0", xT[:].rearrange("p c t -> p (c t)"), [128, 8 * T])

            ffn(0)
            dbgdump("x_l0", xT[:, 0, :], [128, T])
            dbgdump("F_x1", xT[:].rearrange("p c t -> p (c t)"), [128, 8 * T])

            with ExitStack() as esG:
                giv = wview(gm_w_in)
                wr_t = [P.sb(esG, "gws%d" % i, [128, 8, 512], BF16) for i in range(2)]
                P.dma("pool", wr_t[0][:], giv[:, :, 0:512], "gw0")
                P.dma("pool", wr_t[1][:], giv[:, :, 512:1024], "gw1")
                hT = P.sb(esG, "hT", [128, 8, T], BF16)
                with ExitStack() as es3:
                    xnorm(es3, 1, 0, hT)
                uTb = P.sb(esG, "uTb", [128, 8, T], BF16)
                gT = P.sb(esG, "gT", [128, 8, T], BF16)
                vraw = P.sb(esG, "vraw", [128, 12, 1024], BF16)
                vn = vraw
                wsT = P.sb(esG, "wsT", [128, 8, 128], BF16)
                bsbc = P.sb(esG, "bsbc", [128, 1024], F32)
                gvbc = P.sb(esG, "gvbc", [128, 1024], F32)
                ssq = P.sb(esG, "ssq", [128, 24], F32)
                rsv = P.sb(esG, "rsv", [128, 12], F32)
                P.dma("pool", wsT[:], wsT_d, "gmc2")
                P.dma("sp", bsbc[:], bsbc_d, "gmc")
                P.dma("sp", gvbc[:], gvbc_d, "gmc")
                sqv = Ring([P.sb(esG, "sqv%d" % i, [128, 512], BF16) for i in range(3)])
                tb = Ring([P.sb(esG, "gtb%d" % i, [128, 512], F32) for i in range(3)])
                n = 2
                for pc in range(2):
                    sl = wr_t[pc]
                    for mm_ in range(4):
                        m = pc * 4 + mm_
                        for b in range(3):
                            ps = P.bank()
                            P.mmg(ps[:], [(sl[:, kc, mm_ * 128:(mm_ + 1) * 128], hT[:, kc, blk(b)]) for kc in range(8)])
                            P.cp("act", uTb[:, m, blk(b)], ps[:])
                for half in range(2):
                    P.dma("pool", wr_t[half][:], giv[:, :, 1024 + half * 512:1024 + (half + 1) * 512], "gw%d" % half)

                def v_tile(i):
                    for half in range(2):
                        ps = P.bank()
                        P.mmg(ps[:], [(hT[:, kc, i * 128:(i + 1) * 128], wr_t[half][:, kc, :]) for kc in range(8)])
                        P.cp("act", vraw[:, i, half * 512:(half + 1) * 512], ps[:])
                        sq = sqv.next()
                        P.act(sq[:], ps[:], AF.Square)
                        P.op("dve", lambda g, sq=sq, i=i, half=half: g.reduce_sum(out=ssq[:, 2 * i + half:2 * i + half + 1], in_=sq[:], axis=AX.X),
                             [sq[:]], [ssq[:, 2 * i + half:2 * i + half + 1]])
                    r = rsv[:, i:i + 1]
                    P.tt("dve", r, ssq[:, 2 * i:2 * i + 1], ssq[:, 2 * i + 1:2 * i + 2], ALU.add)
                    P.act(r, r, AF.Ln, bias=epsc[:, EPSI[1024]:EPSI[1024] + 1])
                    P.act(r, r, AF.Exp, scale=-0.5, bias=epsc[:, 4:5])
                    P.stt("dve", vn[:, i, :], vraw[:, i, :], r, gvbc[:], ALU.mult, ALU.mult)

                def sp_tile(i):
                    for gh in range(2):
                        ps = P.bank()
                        for gl in range(4):
                            g = gh * 4 + gl
                            P.mm(ps[:, gl * 128:(gl + 1) * 128], vn[:, i, g * 128:(g + 1) * 128], wsT[:, g, :], inc=(gl == 3))
                        t = tb.next()
                        P.tt("dve", t[:], ps[:], bsbc[:, gh * 512:(gh + 1) * 512], ALU.add)
                        P.tt("pool", gT[:, gh * 4:(gh + 1) * 4, i * 128:(i + 1) * 128],
                             t[:].rearrange("p (g q) -> p g q", g=4), uTb[:, gh * 4:(gh + 1) * 4, i * 128:(i + 1) * 128], ALU.mult)

                for i in range(13):
                    if i < 12:
                        v_tile(i)
                    if i >= 1:
                        sp_tile(i - 1)
                dbgdump("gT", gT[:, 0, :], [128, T])
                with ExitStack() as esE:
                    wo = P.sb(esE, "gwo", [128, 8, 1024], BF16)
                    P.dma("pool", wo[:], wview(gm_w_out), "gwo")
                    for b in range(3):
                        for m in range(8):
                            ps = P.bank()
                            P.mmg(ps[:], [(wo[:, kc, m * 128:(m + 1) * 128], gT[:, kc, blk(b)]) for kc in range(8)])
                            residual_update(ps, 1, 2, m, b)
            dbgdump("x_mix1", xT[:, 0, :], [128, T])
            dbgdump("F_x2", xT[:].rearrange("p c t -> p (c t)"), [128, 8 * T])

            ffn(1)

            with ExitStack() as esO:
                ostg = Ring([P.sb(esO, "ostg%d" % i, [128, 1024], F32) for i in range(2)])
                for i in range(12):
                    st = ostg.next()
                    for half in range(2):
                        ps = P.bank()
                        for j in range(4):
                            kc = half * 4 + j
                            P.tr(ps[:, j * 128:(j + 1) * 128], xT[:, kc, i * 128:(i + 1) * 128], ident[:])
                        P.cp("act" if half == 0 else "dve", st[:, half * 512:(half + 1) * 512], ps[:])
                    P.dma("sp", y_out[i * 128:(i + 1) * 128, :], st[:], "yout")
        except _Stop:
            pass
        for k in ("d_yout", "d_stout", "d_dbg"):
            if k in P.sem:
                nc.sync.wait_ge(P.sem[k], P.dtot[k])
        print("instructions:", P.nins, {e: P.cnt[e] for e in P.cnt})
    return nc


def _host_consts():
    ident = np.eye(128, dtype=np.float32)
    L, GW = 1024, 64
    row = np.repeat(np.arange(L // GW), GW).astype(np.float32)
    col = np.tile(np.arange(GW), L // GW).astype(np.float32)
    inv = (1.0 / (np.float32(10000.0) ** (np.arange(0, 16, 2, dtype=np.float32) / np.float32(16)))).astype(np.float32)
    ang = np.concatenate([row[:, None] * inv, col[:, None] * inv], axis=-1).astype(np.float32)
    cs, sn = np.cos(ang).astype(np.float32), np.sin(ang).astype(np.float32)
    C = np.ones((96, L), np.float32)
    S = np.zeros((96, L), np.float32)
    C[64:80] = cs.T
    C[80:96] = cs.T
    S[64:80] = sn.T
    S[80:96] = sn.T
    rope = np.stack([C, S]).astype(np.float32)
    srot = np.zeros((96, 96), np.float32)
    for i in range(16):
        srot[80 + i, 64 + i] = -1.0
        srot[64 + i, 80 + i] = 1.0
    epl = np.zeros((32, 96), np.float32)
    for i in range(32):
        epl[i, 64 + i] = 1.0
    invcnt = np.zeros((4, T), np.float32)
    for g, w in enumerate((2, 4, 8, 16)):
        for (t0, Ls) in SEQS:
            t = np.arange(Ls)
            lo = np.clip(t - w // 2, 0, Ls)
            hi = np.clip(t + w // 2, 0, Ls)
            invcnt[g, t0:t0 + Ls] = (1.0 / (hi - lo).astype(np.float32)).astype(np.float32)
    return ident, rope, srot, epl, invcnt


def _fm(v):
    return np.ascontiguousarray(np.asarray(v, np.float32).reshape(-1, 128).T)


def make_in_maps(inp):
    ident, rope, srot, epl, invcnt = _host_consts()
    f = lambda k: np.ascontiguousarray(np.asarray(inp[k], dtype=np.float32))
    x_prompt, x_sample = f("x_prompt"), f("x_sample")
    cache_ckv, cache_krope, c, c_ctx = f("cache_ckv"), f("cache_krope"), f("c"), f("c_ctx")
    shared = dict(
        ident=ident, rope=rope, srot=srot, eplace=epl, invcnt=invcnt,
        bsbc=np.ascontiguousarray(np.broadcast_to(f("gm_bs")[0].reshape(1, 1024), (128, 1024))),
        gvbc=np.ascontiguousarray(np.broadcast_to(f("gm_vnorm_g")[0].reshape(1, 1024), (128, 1024))),
        wsT=np.ascontiguousarray(f("gm_ws")[0].transpose(2, 0, 1)),
        ada_w=f("ada_w"), ffn_wg=f("ffn_wg"), ffn_wu=f("ffn_wu"), ffn_wd=f("ffn_wd"),
        ab_w_in=f("ab_w_in")[0], pool_w=f("pool_w")[0], w_uq=f("w_uq")[0], w_ukv=f("w_ukv")[0],
        ab_w_out=f("ab_w_out")[0], gm_w_in=f("gm_w_in")[0], gm_w_out=f("gm_w_out")[0],
    )
    vbase = np.zeros((128, NV), np.float32)
    nmg, nfg, adab = f("norm_mix_g"), f("norm_ffn_g"), f("ada_b")
    for l in range(2):
        vbase[:, l * 8:(l + 1) * 8] = _fm(nmg[l])
        vbase[:, 16 + l * 8:16 + (l + 1) * 8] = _fm(nfg[l])
        vbase[:, 32 + l * 48:32 + (l + 1) * 48] = _fm(adab[l])
    vbase[:, 128:132] = _fm(f("pool_scale")[0])
    vbase[:, 132:135] = _fm(f("q_norm_g")[0])
    vbase[:, 135:137] = _fm(f("kv_norm_g")[0])
    vbase[0:96, 137] = f("qn_g")[0]
    vbase[0:96, 138] = f("kn_g")[0]
    vbase[:, 139:147] = _fm(f("gm_vnorm_g")[0])
    cc = _fm(c_ctx)
    maps = []
    for i in range(8):
        v = vbase.copy()
        v[:, 147:163:2] = cc
        v[:, 148:163:2] = _fm(c[i])
        m = dict(shared)
        m["x_in"] = np.ascontiguousarray(np.concatenate([x_prompt[2 * i], x_prompt[2 * i + 1], x_sample[i]], axis=0))
        m["cckv"] = np.ascontiguousarray(cache_ckv[i, 0])
        m["ckr"] = np.ascontiguousarray(cache_krope[i, 0])
        m["vecs"] = v
        maps.append(m)
    return maps


_NC_CACHE = {}


def kernel(**inputs):
    if "nc" not in _NC_CACHE:
        _NC_CACHE["nc"] = build()
    nc = _NC_CACHE["nc"]
    maps = make_in_maps(inputs)
    res = run_bass_kernel_spmd(nc, maps, core_ids=list(range(8)))
    y_prompt = np.zeros((16, 256, 1024), np.float32)
    y_sample = np.zeros((8, 1024, 1024), np.float32)
    sckv = np.zeros((16, 1, 256, 256), np.float32)
    skr = np.zeros((16, 1, 256, 32), np.float32)
    for i, r in enumerate(res.results):
        y = np.asarray(r["y"], np.float32)
        y_prompt[2 * i] = y[0:256]
        y_prompt[2 * i + 1] = y[256:512]
        y_sample[i] = y[512:1536]
        a = np.asarray(r["sckv"], np.float32)
        b = np.asarray(r["skr"], np.float32)
        sckv[2 * i, 0] = a[0:256]
        sckv[2 * i + 1, 0] = a[256:512]
        skr[2 * i, 0] = b[0:256]
        skr[2 * i + 1, 0] = b[256:512]
    return (y_prompt, y_sample, sckv, skr)
```
